# Optimizing a Trainium2 kernel written in Bass

```python
import jax, jax.numpy as jnp
from jax import lax
import numpy as np

D_MODEL = 1024
BATCH = 2
SEQ = 8192
DEPTH = 1

D_CONV = D_MODEL
CONV_WIDTH = 3
HEAD_DIM = 64
HEADS_PER_GROUP = 8
GROUPS = ((128, 1), (512, 4), (2048, 16))
N_GROUPS = len(GROUPS)
N_ATT_HEADS = N_GROUPS * HEADS_PER_GROUP
ATT_GROUP_W = HEADS_PER_GROUP * HEAD_DIM
ATT_QKV_W = N_GROUPS * ATT_GROUP_W
D_FF = 2816
LN_EPS = 1e-5
ALPHA = (2.0 * DEPTH) ** 0.25
BETA = (8.0 * DEPTH) ** -0.25
MASK_VALUE = -1e30

OFF_B = 0
OFF_C = OFF_B + D_CONV
OFF_H = OFF_C + D_CONV
OFF_Q = OFF_H + D_CONV
OFF_K = OFF_Q + ATT_QKV_W
OFF_V = OFF_K + ATT_QKV_W
OFF_GA = OFF_V + ATT_QKV_W
OFF_GB = OFF_GA + D_MODEL
N_IN = OFF_GB + D_MODEL

kernel_name = "hybrid_shortconv_dilated_alibi_deepnorm_encoder"


def layer_norm(x, g, b):
    xf = x.astype(jnp.float32)
    mu = jnp.mean(xf, -1, keepdims=True)
    xc = xf - mu
    var = jnp.mean(xc * xc, -1, keepdims=True)
    return (xc * lax.rsqrt(var + LN_EPS) * g + b).astype(x.dtype)


def dwconv3(u, w):
    up = jnp.pad(u, ((0, 0), (1, 1), (0, 0)))
    return up[:, :-2] * w[0] + up[:, 1:-1] * w[1] + up[:, 2:] * w[2]


def alibi_slopes(n):
    return jnp.exp2(-8.0 * jnp.arange(1, n + 1, dtype=jnp.float32) / n)


def dilated_window_attention(q, k, v, dil, radius, slopes):
    bsz, seq, nh, hd = q.shape
    sub_len = seq // dil
    blk = radius
    nb = -(-sub_len // blk)
    lp = nb * blk

    def to_sub(a):
        a = a.reshape(bsz, sub_len, dil, nh, hd).transpose(0, 2, 1, 3, 4)
        return a.reshape(bsz * dil, sub_len, nh, hd)

    qs, ks, vs = to_sub(q), to_sub(k), to_sub(v)
    qb = jnp.pad(qs, ((0, 0), (0, lp - sub_len), (0, 0), (0, 0))).reshape(bsz * dil, nb, blk, nh, hd)

    def windows(a):
        ap = jnp.pad(a, ((0, 0), (blk, lp - sub_len + blk), (0, 0), (0, 0)))
        ap = ap.reshape(bsz * dil, nb + 2, blk, nh, hd)
        return jnp.concatenate([ap[:, :-2], ap[:, 1:-1], ap[:, 2:]], axis=2)

    kw, vw = windows(ks), windows(vs)
    qpos = jnp.arange(lp).reshape(nb, blk)
    kpos = (jnp.arange(nb)[:, None] - 1) * blk + jnp.arange(3 * blk)[None, :]
    rel = kpos[:, None, :] - qpos[:, :, None]
    valid = (jnp.abs(rel) <= radius) & (kpos[:, None, :] >= 0) & (kpos[:, None, :] < sub_len)
    dist = (jnp.abs(rel) * dil).astype(jnp.float32)
    bias = -slopes[None, :, None, None] * dist[:, None]
    s = jnp.einsum('bnqhd,bnkhd->bnhqk', qb, kw).astype(jnp.float32) * (hd ** -0.5) + bias
    s = jnp.where(valid[:, None], s, MASK_VALUE)
    m = jnp.max(s, -1, keepdims=True)
    p = jnp.exp(s - m)
    den = jnp.sum(p, -1, keepdims=True)
    o = jnp.einsum('bnhqk,bnkhd->bnqhd', (p / den).astype(v.dtype), vw)
    lse = jnp.transpose((m + jnp.log(den))[..., 0], (0, 1, 3, 2))
    o = o.reshape(bsz, dil, lp, nh, hd)[:, :, :sub_len].transpose(0, 2, 1, 3, 4).reshape(bsz, seq, nh, hd)
    lse = lse.reshape(bsz, dil, lp, nh)[:, :, :sub_len].transpose(0, 2, 1, 3).reshape(bsz, seq, nh)
    return o, lse


def setup_inputs(seed: int = 0) -> dict:
    key = jax.random.key(seed)
    ks = jax.random.split(key, 24)

    def nrm(k, shape, scale):
        return jax.random.normal(k, shape, jnp.float32) * scale

    col_scale = np.ones((N_IN,), np.float32)
    col_scale[OFF_H:OFF_H + D_CONV] = BETA
    col_scale[OFF_V:OFF_V + ATT_QKV_W] = BETA
    L = DEPTH
    return {
        "x": nrm(ks[0], (BATCH, SEQ, D_MODEL), 1.0),
        "ln0_g": 1.0 + nrm(ks[1], (D_MODEL,), 0.02),
        "ln0_b": nrm(ks[2], (D_MODEL,), 0.02),
        "w_in": nrm(ks[3], (L, D_MODEL, N_IN), D_MODEL ** -0.5) * jnp.asarray(col_scale),
        "b_in": nrm(ks[4], (L, N_IN), 0.02),
        "conv_w": nrm(ks[5], (L, CONV_WIDTH, D_CONV), CONV_WIDTH ** -0.5),
        "w_a": nrm(ks[6], (L, D_CONV, D_MODEL), BETA * D_CONV ** -0.5),
        "w_b": nrm(ks[7], (L, ATT_GROUP_W, D_MODEL), BETA * ATT_GROUP_W ** -0.5),
        "w_o": nrm(ks[8], (L, D_MODEL, D_MODEL), BETA * D_MODEL ** -0.5),
        "b_o": nrm(ks[9], (L, D_MODEL), 0.02),
        "ln1_g": 1.0 + nrm(ks[10], (L, D_MODEL), 0.02),
        "ln1_b": nrm(ks[11], (L, D_MODEL), 0.02),
        "w_up": nrm(ks[12], (L, D_MODEL, 2 * D_FF), BETA * D_MODEL ** -0.5),
        "b_up": nrm(ks[13], (L, 2 * D_FF), 0.02),
        "ffn_conv_w": nrm(ks[14], (L, CONV_WIDTH, D_FF), CONV_WIDTH ** -0.5),
        "ffn_conv_b": nrm(ks[15], (L, D_FF), 0.02),
        "w_down": nrm(ks[16], (L, D_FF, D_MODEL), BETA * D_FF ** -0.5),
        "b_down": nrm(ks[17], (L, D_MODEL), 0.02),
        "ln2_g": 1.0 + nrm(ks[18], (L, D_MODEL), 0.02),
        "ln2_b": nrm(ks[19], (L, D_MODEL), 0.02),
    }


def reference(x, ln0_g, ln0_b, w_in, b_in, conv_w, w_a, w_b, w_o, b_o, ln1_g, ln1_b,
              w_up, b_up, ffn_conv_w, ffn_conv_b, w_down, b_down, ln2_g, ln2_b):
    bsz, seq, _ = x.shape
    slopes = alibi_slopes(N_ATT_HEADS).reshape(N_GROUPS, HEADS_PER_GROUP)
    h = layer_norm(x, ln0_g, ln0_b)
    for l in range(DEPTH):
        proj = h @ w_in[l] + b_in[l]
        gate_b = proj[..., OFF_B:OFF_B + D_CONV]
        gate_c = proj[..., OFF_C:OFF_C + D_CONV]
        hin = proj[..., OFF_H:OFF_H + D_CONV]
        y_a = (gate_b * dwconv3(gate_c * hin, conv_w[l])) @ w_a[l]
        q = proj[..., OFF_Q:OFF_Q + ATT_QKV_W].reshape(bsz, seq, N_GROUPS, HEADS_PER_GROUP, HEAD_DIM)
        k = proj[..., OFF_K:OFF_K + ATT_QKV_W].reshape(bsz, seq, N_GROUPS, HEADS_PER_GROUP, HEAD_DIM)
        v = proj[..., OFF_V:OFF_V + ATT_QKV_W].reshape(bsz, seq, N_GROUPS, HEADS_PER_GROUP, HEAD_DIM)
        outs, lses = [], []
        for g, (window, dil) in enumerate(GROUPS):
            o, lse = dilated_window_attention(q[:, :, g], k[:, :, g], v[:, :, g], dil,
                                              window // (2 * dil), slopes[g])
            outs.append(o)
            lses.append(lse)
        wts = jax.nn.softmax(jnp.stack(lses, 0), axis=0)
        comb = jnp.sum(wts[..., None].astype(x.dtype) * jnp.stack(outs, 0), axis=0)
        y_b = comb.reshape(bsz, seq, ATT_GROUP_W) @ w_b[l]
        g_a = jax.nn.sigmoid(proj[..., OFF_GA:OFF_GA + D_MODEL])
        g_b = jax.nn.sigmoid(proj[..., OFF_GB:OFF_GB + D_MODEL])
        mix = (g_a * y_a + g_b * y_b) @ w_o[l] + b_o[l]
        h = layer_norm(ALPHA * h + mix, ln1_g[l], ln1_b[l])
        up = h @ w_up[l] + b_up[l]
        a, gte = up[..., :D_FF], up[..., D_FF:]
        f = jax.nn.gelu(dwconv3(a, ffn_conv_w[l]) + ffn_conv_b[l], approximate=False) * gte
        ffn = f @ w_down[l] + b_down[l]
        h = layer_norm(ALPHA * h + ffn, ln2_g[l], ln2_b[l])
    return h
```

```python
import numpy as np
from contextlib import ExitStack
import concourse.bass as bass
import concourse.mybir as mybir
from concourse.bass_utils import run_bass_kernel_spmd

F32 = mybir.dt.float32
BF16 = mybir.dt.bfloat16
AF = mybir.ActivationFunctionType
ALU = mybir.AluOpType

D = 1024
NIN = 9728
DFF = 2816
SEQ = 8192
NCORE = 8
TPC = 2048
NE = 2050
HL = 1025
NXB = 33
NX = NXB * 128
OFF_B, OFF_C, OFF_H, OFF_Q, OFF_K, OFF_V, OFF_GA, OFF_GB = 0, 1024, 2048, 3072, 4608, 6144, 7680, 8704
GROUPS = ((128, 1), (512, 4), (2048, 16))
ALPHA = 2.0 ** 0.25
LN_EPS = 1e-5
SLOPES = [float(2.0 ** (-8.0 * (i + 1) / 24.0)) for i in range(24)]

CP_BIN = 0
CP_CW = 76
CP_BUP = 100
CP_FCW = 144
CP_FCB = 210
CP_FLAG = 232
CP_VAL = 234


def class_plan():
    plan = []
    vcol = 0
    for g, (win, d) in enumerate(GROUPS):
        for r in range(d):
            ia = -((1 + r) // d)
            ib = (2048 - r) // d + 1
            ntile = (ib - ia + 127) // 128 + 1
            tiles = []
            for j in range(ntile):
                ks = ia - 64 + 128 * j
                ke = min(ks + 128, ib + 64)
                tiles.append((ks, ke, vcol))
                vcol += 1
            plan.append((g, d, r, ia, ib, tiles))
    return plan, vcol


import os as _os
_MODE = int(_os.environ.get('MK_MODE', '0'))
PLAN, NVT = class_plan()
SB_BASE = 16512
SB_LIMIT = 229376
NCP = CP_VAL + NVT + 512 + 1 + 16
CP_D = CP_VAL + NVT
CP_M = CP_D + 256
CP_EPS = CP_M + 256
CP_G0 = CP_EPS + 1
CP_B0 = CP_G0 + 8


class Buf:
    __slots__ = ("name", "w", "r")

    def __init__(self, name):
        self.name = name
        self.w = None
        self.r = {}


class Eng:
    def __init__(self, name, sem):
        self.name = name
        self.sem = sem
        self.count = 0
        self.seen = {}


class Sched:
    def __init__(self, nc, stack):
        self.nc = nc
        self.stack = stack
        self.engs = {}
        for nm in ("pe", "act", "dve", "pool", "sp"):
            sem = stack.enter_context(nc.semaphore("s_" + nm))
            self.engs[nm] = Eng(nm, sem)
        self.chans = {}
        self.prog = {nm: [] for nm in self.engs}

    def chan(self, name):
        if name not in self.chans:
            sem = self.stack.enter_context(self.nc.semaphore("d_" + name))
            self.chans[name] = Eng("dma_" + name, sem)
        return self.chans[name]

    def _deps(self, eng, reads, writes):
        deps = {}
        for b in reads:
            if b.w is not None:
                e, t = b.w
                if deps.get(e, 0) < t:
                    deps[e] = t
        for b in writes:
            if b.w is not None:
                e, t = b.w
                if deps.get(e, 0) < t:
                    deps[e] = t
            for e, t in b.r.items():
                if deps.get(e, 0) < t:
                    deps[e] = t
        for e, t in deps.items():
            if e is eng and eng.name in ("pe", "sp"):
                continue
            if eng.seen.get(e, 0) < t:
                self.prog[eng.name].append(("wait", e.sem, t))
                eng.seen[e] = t

    def op(self, en, fn, reads=(), writes=()):
        eng = self.engs[en]
        self._deps(eng, reads, writes)
        eng.count += 1
        self.prog[en].append(("op", fn, eng.sem, 1))
        t = eng.count
        for b in reads:
            b.r[eng] = t
        for b in writes:
            b.w = (eng, t)
            b.r = {}

    def dma(self, qn, chan, out, in_, reads=(), writes=()):
        q = self.engs[qn]
        ch = self.chan(chan + "_" + (writes[0].name if writes else reads[0].name))
        self._deps(q, reads, writes)
        ch.count += 16
        self.prog[qn].append(("op", (lambda e, out=out, in_=in_: e.dma_start(out=out, in_=in_)), ch.sem, 16))
        t = ch.count
        for b in reads:
            b.r[ch] = t
        for b in writes:
            b.w = (ch, t)
            b.r = {}

    def barrier(self):
        allc = list(self.engs.values()) + list(self.chans.values())
        for eng in self.engs.values():
            for o in allc:
                if o is eng or o.count == 0:
                    continue
                if eng.seen.get(o, 0) < o.count:
                    self.prog[eng.name].append(("wait", o.sem, o.count))
                    eng.seen[o] = o.count

    def emit(self):
        nc = self.nc
        with nc.Block() as block:
            def runner(nm):
                def _(e):
                    for it in self.prog[nm]:
                        if it[0] == "wait":
                            e.wait_ge(it[1], it[2])
                        else:
                            it[1](e).then_inc(it[2], it[3])
                return _
            block.tensor(runner("pe"))
            block.scalar(runner("act"))
            block.vector(runner("dve"))
            block.gpsimd(runner("pool"))
            block.sync(runner("sp"))


class T:
    def __init__(self, t, name):
        self.t = t
        self.b = Buf(name)


def etiles(n, step=512):
    return [(a, min(step, n - a)) for a in range(0, n, step)]


def build_program(debug=False):
    nc = bass.Bass("TRN2", target_bir_lowering=False)
    dbg = {}

    def din(name, shape, dtype=F32):
        return nc.dram_tensor(name, list(shape), dtype, kind="ExternalInput").ap()

    x_ext = din("x_ext", [NX, D])
    w_in = din("w_in", [D, NIN])
    w_a = din("w_a", [D, D])
    w_b = din("w_b", [512, D])
    w_o = din("w_o", [D, D])
    w_up = din("w_up", [D, 2 * DFF])
    w_down = din("w_down", [DFF, D])
    cpart = din("cpart", [128, NCP])
    ident_in = din("ident", [128, 128])
    bc = {n: din("bc_" + n, [128, w]) for n, w in
          (("ln0_g", D), ("ln0_b", D), ("b_v", 1536), ("b_o", D), ("ln1_g", D), ("ln1_b", D),
           ("b_down", D), ("ln2_g", D), ("ln2_b", D))}
    out = nc.dram_tensor("out", [TPC, D], F32, kind="ExternalOutput").ap()
    zn_scr = nc.dram_tensor("zn_scr", [17 * 128, D], F32, kind="ExternalOutput" if debug else "Internal").ap()
    h_scr = nc.dram_tensor("h_scr", [17 * 128, D], F32, kind="Internal").ap()
    if debug:
        for n_, shp_ in (("d_qT", [128, 3, NE]), ("d_kT", [128, 2178 + 2562 + 4098]), ("d_vT", [128, NVT, 128])):
            dbg[n_] = nc.dram_tensor(n_, shp_, BF16, kind="ExternalOutput").ap()
        for n_, shp_ in (("d_accn", [128, NE]), ("d_accd", [128, NE]), ("d_Wm", [128, 6, 256])):
            dbg[n_] = nc.dram_tensor(n_, shp_, F32, kind="ExternalOutput").ap()
        for n_, shp_ in (("d_hT", [128, 8, NX]), ("d_comb", [128, 4, NE]), ("d_yap", [128, 8, NE]), ("d_mg", [128, 8, NE]), ("d_h1T", [128, 8, NE])):
            dbg[n_] = nc.dram_tensor(n_, shp_, BF16, kind="ExternalOutput").ap()

    def dump(S, name, t):
        if debug and name in dbg:
            S.dma("sp", "dbg", dbg[name][:], t.t[:], reads=[t.b])

    w_in_v = w_in.rearrange("(k p) n -> p k n", p=128)
    w_a_v = w_a.rearrange("(k p) n -> p k n", p=128)
    w_b_v = w_b.rearrange("(k p) n -> p k n", p=128)
    w_o_v = w_o.rearrange("(k p) n -> p k n", p=128)
    w_up_v = w_up.rearrange("(k p) n -> p k n", p=128)
    w_down_v = w_down.rearrange("(k p) n -> p k n", p=128)

    with ExitStack() as st:
        S = Sched(nc, st)

        uid = [0]

        def at(name, shape, dtype, off):
            uid[0] += 1
            nb = int(np.prod(shape[1:])) * (2 if dtype == BF16 else 4)
            assert off % 32 == 0 and off + nb <= SB_LIMIT, (name, off, nb)
            return T(nc.alloc_sbuf_tensor_at("%s_%d" % (name, uid[0]), list(shape), dtype, offset=off), name)

        class Region:
            def __init__(self, lo, hi):
                self.lo, self.hi, self.cur = lo, hi, lo

            def reset(self):
                self.cur = self.lo

            def sb(self, name, shape, dtype=F32):
                nb = int(np.prod(shape[1:])) * (2 if dtype == BF16 else 4)
                nb = (nb + 31) // 32 * 32
                assert self.cur + nb <= self.hi, (name, self.cur, nb, self.hi)
                t = at(name, shape, dtype, self.cur)
                self.cur += nb
                return t

        PB_ = SB_BASE + 37376
        PERS = Region(SB_BASE, PB_)
        RR = Region(PB_, PB_ + 67584)
        RA = Region(PB_ + 67584, PB_ + 135168)
        RC = Region(PB_ + 135168, PB_ + 151584)
        RL = Region(PB_ + 151584, SB_LIMIT)
        R2 = Region(PB_ + 67584, SB_LIMIT)
        sb = PERS.sb

        def ps(name, shape, dtype=F32):
            return T(st.enter_context(nc.psum_tensor(name, list(shape), dtype)), name)

        regA = RA.sb("regA", [128, 8, NX], BF16)
        cp = sb("cp", [128, NCP])
        ident = sb("ident", [128, 128], BF16)
        ones = sb("ones", [128, 64], BF16)
        bcA = sb("bcA", [128, 4, D])
        wbuf = [sb("wbuf%d" % i, [128, 8, 128], BF16) for i in range(8)]
        ident_f = RL.sb("ident_f", [128, 128])
        NST = 4
        st6 = [sb("st6_%d" % i, [128, 2, 6]) for i in range(NST)]
        mv = [sb("mv%d" % i, [128, 2]) for i in range(NST)]
        rs = [sb("rs%d" % i, [128, 1]) for i in range(NST)]
        nm = [sb("nm%d" % i, [128, 1]) for i in range(NST)]
        P = [ps("ps%d" % i, [128, 512]) for i in range(8)]

        class _PTv:
            def __init__(self, p):
                self.t = p.t[:, :].bitcast(BF16).rearrange("p (a b) -> p a b", a=8)
                self.b = p.b

        PT = [_PTv(P[6]), _PTv(P[7])]

        hT = regA.t
        S.dma("sp", "c", cp.t[:], cpart[:], writes=[cp.b])
        S.dma("sp", "c", ident_f.t[:], ident_in[:], writes=[ident_f.b])
        S.op("dve", lambda e: e.tensor_copy(ident.t[:], ident_f.t[:]), reads=[ident_f.b], writes=[ident.b])
        S.op("pool", lambda e: e.memset(ones.t[:], 1.0), writes=[ones.b])
        S.dma("sp", "c", bcA.t[:, 0, :], bc["ln0_g"][:], writes=[bcA.b])
        S.dma("sp", "c", bcA.t[:, 1, :], bc["ln0_b"][:], writes=[bcA.b])
        S.dma("sp", "c", bcA.t[:, 3, :], bc["b_o"][:], writes=[bcA.b])
        S.op("dve", lambda e: e.tensor_scalar(bcA.t[:, 2, :], bcA.t[:, 0, :], ALPHA, None, ALU.mult), reads=[bcA.b], writes=[bcA.b])
        S.op("dve", lambda e: e.scalar_tensor_tensor(bcA.t[:, 3, :], bcA.t[:, 1, :], ALPHA, bcA.t[:, 3, :], ALU.mult, ALU.add), reads=[bcA.b], writes=[bcA.b])

        cpt = cp.t

        def cpc(col):
            return cpt[:, col:col + 1]

        WSEQ = []
        for pair_ in range(4):
            for g_ in range(3):
                for o_ in (OFF_Q, OFF_K, OFF_V):
                    WSEQ.append((w_in_v, o_ + g_ * 512 + pair_ * 128, 8))
        for i_ in range(8):
            for o_ in (OFF_C, OFF_H, OFF_B):
                WSEQ.append((w_in_v, o_ + i_ * 128, 8))
        for i_ in range(8):
            WSEQ += [(w_in_v, OFF_GA + i_ * 128, 8), (w_in_v, OFF_GB + i_ * 128, 8), (w_a_v, i_ * 128, 8), (w_b_v, i_ * 128, 4)]
        for h_ in range(2):
            for i_ in range(22):
                WSEQ += [(w_up_v, i_ * 128, 8), (w_up_v, DFF + i_ * 128, 8)]
        wst = {"issued": 0, "taken": 0}
        LOOK = 4

        def issue_w(n):
            while wst["issued"] < min(n, len(WSEQ)):
                view, c0, nk = WSEQ[wst["issued"]]
                wb = wbuf[wst["issued"] % len(wbuf)]
                S.dma("pool", "w", wb.t[:, 0:nk, :], view[:, 0:nk, c0:c0 + 128], writes=[wb.b])
                wst["issued"] += 1

        def load_w(view, c0, nk=8):
            ent = WSEQ[wst["taken"]]
            assert ent[0] is view and ent[1] == c0 and ent[2] == nk, (wst["taken"], c0, nk)
            issue_w(wst["taken"] + 1 + LOOK)
            wb = wbuf[wst["taken"] % len(wbuf)]
            wst["taken"] += 1
            return wb

        issue_w(LOOK)

        psrot = [0]
        gen_banks = [0, 1, 2, 3, 4, 5]

        def getps():
            p = P[gen_banks[psrot[0] % len(gen_banks)]]
            psrot[0] += 1
            return p

        def run_pipeline(n, stages):
            ns = len(stages)
            for t_ in range(n + ns - 1):
                for k_, f in enumerate(stages):
                    b_ = t_ - k_
                    if 0 <= b_ < n:
                        f(b_)

        def ln_stats_a(src, M, i):
            i %= NST
            for a_ in range(2):
                S.op("dve", lambda e, a_=a_: e.bn_stats(st6[i].t[0:M, a_, :], src.t[0:M, a_ * 512:(a_ + 1) * 512]),
                     reads=[src.b], writes=[st6[i].b])
            S.op("dve", lambda e: e.bn_aggr(mv[i].t[0:M, :], st6[i].t[0:M, :, :].rearrange("p a f -> p (a f)")),
                 reads=[st6[i].b], writes=[mv[i].b])
            S.op("act", lambda e: e.activation(rs[i].t[0:M, :], mv[i].t[0:M, 1:2], AF.Sqrt, bias=cpt[0:M, CP_EPS:CP_EPS + 1], scale=1.0),
                 reads=[mv[i].b, cp.b], writes=[rs[i].b])

        def ln_stats_b(src, dst, M, i):
            i %= NST
            S.op("dve", lambda e: e.reciprocal(rs[i].t[0:M, :], rs[i].t[0:M, :]), reads=[rs[i].b], writes=[rs[i].b])
            S.op("dve", lambda e: e.scalar_tensor_tensor(nm[i].t[0:M, :], mv[i].t[0:M, 0:1], -1.0, rs[i].t[0:M, :], ALU.mult, ALU.mult),
                 reads=[mv[i].b, rs[i].b], writes=[nm[i].b])
            S.op("act", lambda e: e.activation(dst.t[0:M, :], src.t[0:M, :], AF.Identity, bias=nm[i].t[0:M, 0:1], scale=rs[i].t[0:M, 0:1]),
                 reads=[src.b, rs[i].b, nm[i].b], writes=[dst.b])

        def transpose_to(srcbf, M, dst_ap, dst_buf, i):
            pt = PT[i % 2]
            for k in range(8):
                S.op("pe", lambda e, k=k: e.transpose(pt.t[:, k, 0:M], srcbf.t[0:M, k * 128:(k + 1) * 128], ident.t[0:M, 0:M]),
                     reads=[srcbf.b, ident.b], writes=[pt.b])
            S.op("act", lambda e: e.activation(dst_ap, pt.t[:, :, 0:M], AF.Identity), reads=[pt.b], writes=[dst_buf])

        hTb = [Buf("hT%d" % i) for i in range(NXB)]

        def hbufs(c0, n):
            return hTb[c0 // 128:(c0 + n - 1) // 128 + 1]

        def gen_p1(blocks, xin, xnb, xn=None, tmp1=None, hf=None):
            nb_ = len(blocks)

            def s0(i):
                b = blocks[i]
                S.dma("sp", "x", xin[i % len(xin)].t[:, :], x_ext[b * 128:(b + 1) * 128, :], writes=[xin[i % len(xin)].b])

            def s1(i):
                ln_stats_a(xin[i % len(xin)], 128, i)

            def s2(i):
                b = blocks[i]
                ln_stats_b(xin[i % len(xin)], xnb[i % len(xnb)], 128, i)
                if 8 <= b <= 24:
                    j = i % NST
                    xi, xo = xin[i % len(xin)], xn[i % 2]
                    S.op("act", lambda e: e.activation(xo.t[:, :], xi.t[:, :], AF.Identity, bias=nm[j].t[:, 0:1], scale=rs[j].t[:, 0:1]),
                         reads=[xi.b, rs[j].b, nm[j].b], writes=[xo.b])

            def s3(i):
                b = blocks[i]
                pt = PT[i % 2]
                src = xnb[i % len(xnb)]
                for k in range(8):
                    S.op("pe", lambda e, k=k: e.transpose(pt.t[:, k, :], src.t[:, k * 128:(k + 1) * 128], ident.t[:, :]),
                         reads=[src.b, ident.b], writes=[pt.b])
                if 8 <= b <= 24:
                    t1, xo, f_ = tmp1[i % 2], xn[i % 2], hf[i % 2]
                    S.op("dve", lambda e: e.tensor_tensor(t1.t[:, :], xo.t[:, :], bcA.t[:, 2, :], ALU.mult), reads=[xo.b, bcA.b], writes=[t1.b])
                    S.op("pool", lambda e: e.tensor_tensor(f_.t[:, :], t1.t[:, :], bcA.t[:, 3, :], ALU.add), reads=[t1.b, bcA.b], writes=[f_.b])
                    S.dma("sp", "hs", h_scr[(b - 8) * 128:(b - 7) * 128, :], f_.t[:, :], reads=[f_.b])

            def s4(i):
                b = blocks[i]
                pt = PT[i % 2]
                for k in range(8):
                    dst = hT[:, k, b * 128:(b + 1) * 128]
                    if i % 2 == 0:
                        S.op("act", lambda e, k=k, dst=dst: e.activation(dst, pt.t[:, k, :], AF.Identity, bias=cpc(CP_B0 + k), scale=cpc(CP_G0 + k)),
                             reads=[pt.b, cp.b], writes=[hTb[b]])
                    else:
                        S.op("dve", lambda e, k=k, dst=dst: e.tensor_scalar(dst, pt.t[:, k, :], cpc(CP_G0 + k), cpc(CP_B0 + k), ALU.mult, ALU.add),
                             reads=[pt.b, cp.b], writes=[hTb[b]])

            def s34(i):
                s3(i)
                s4(i)

            stages = [s0, s1, s2, s34]
            for t_ in range(nb_ + len(stages) - 1):
                for k_, f in enumerate(stages):
                    i_ = t_ - k_
                    if 0 <= i_ < nb_:
                        f(i_)
                yield

        p1a_xin = [RR.sb("xin%d" % i, [128, D]) for i in range(4)]
        p1a_xnb = [RR.sb("xnb%d" % i, [128, D], BF16) for i in range(3)]
        p1a_xn = [RR.sb("xn%d" % i, [128, D]) for i in range(2)]
        p1a_t1 = [RR.sb("tmp1_%d" % i, [128, D]) for i in range(2)]
        p1a_hf = [RR.sb("hf%d" % i, [128, D]) for i in range(2)]
        for _ in gen_p1(list(range(7, 25)), p1a_xin, p1a_xnb, p1a_xn, p1a_t1, p1a_hf):
            pass

        def proj_fm(wb, nk, rhs_fn, rhs_bufs, tiles, evac, step=None):
            for (c0, n) in tiles:
                p = getps()
                rb = rhs_bufs(c0, n) if callable(rhs_bufs) else rhs_bufs
                for k in range(nk):
                    S.op("pe", lambda e, k=k, c0=c0, n=n, p=p: e.matmul(p.t[:, 0:n], lhsT=wb.t[:, k, :], rhs=rhs_fn(k, c0, n),
                                                                      start=(k == 0), stop=(k == nk - 1)),
                         reads=[wb.b] + list(rb), writes=[p.b])
                evac(p, c0, n)

        def proj_fm_g(wb, nk, rhs_fn, rhs_bufs, tiles, evac):
            for tl in tiles:
                proj_fm(wb, nk, rhs_fn, rhs_bufs, [tl], evac)
                yield

        S.barrier()
        RR.reset()
        RL.reset()
        RC.reset()
        sb = RR.sb
        NKG = [NE + 128 * d for (_, d) in GROUPS]
        NTG = [sum(len(c[5]) for c in PLAN if c[0] == g) for g in range(3)]
        VC0 = [0, NTG[0], NTG[0] + NTG[1]]
        qTg = [sb("qT%d" % g, [128, NE], BF16) for g in range(3)]
        kTg = [sb("kT%d" % g, [128, NKG[g]], BF16) for g in range(3)]
        vTg = [sb("vT%d" % g, [128, NTG[g], 128], BF16) for g in range(3)]
        accn = sb("accn", [128, NE])
        accd = sb("accd", [128, NE])
        Ebuf = [RL.sb("E%d" % i, [128, 256]) for i in range(4)]
        Pbuf = [RL.sb("Pt%d" % i, [128, 256], BF16) for i in range(8)]
        Wmg = [RL.sb("Wm%d" % g, [128, 2, 256]) for g in range(3)]
        vtt = RL.sb("vtt", [128, 4098], BF16)
        p1b_xin = [RC.sb("xinb%d" % i, [128, D]) for i in range(3)]
        p1b_xnb = [RC.sb("xnbb%d" % i, [128, D], BF16) for i in range(2)]
        RC.reset()
        comb = RC.sb("comb", [128, 4, NE], BF16)
        SA = [P[0], P[2]]
        SBk = [P[1], P[3]]
        PNb = [P[4]]
        PDb = [P[5]]
        gen_banks[:] = [6, 7]
        erot = [0]
        prot = [0]
        ptr = [0]
        gctr = [0]

        def gen_proj(pair, g, dry=False):
            win, d = GROUPS[g]
            cq = OFF_Q + g * 512 + pair * 128
            ck = OFF_K + g * 512 + pair * 128
            cv = OFF_V + g * 512 + pair * 128
            nkc = NKG[g]
            kc0 = 1024 - 64 * d
            classes = [c for c in PLAN if c[0] == g]
            if dry:
                n_ = len(etiles(NE)) + 2 * len(etiles(nkc)) + sum((len(c[5]) + 7) // 8 for c in classes)
                for _ in range(n_):
                    yield
                return
            wq_ = load_w(w_in_v, cq)
            yield from proj_fm_g(wq_, 8, lambda k, c0, n: hT[:, k, 1024 + c0:1024 + c0 + n], lambda c0, n: hbufs(1024 + c0, n), etiles(NE),
                                 lambda p, c0, n: S.op("act", lambda e: e.activation(qTg[g].t[:, c0:c0 + n], p.t[:, 0:n], AF.Identity, bias=cpc(CP_BIN + cq // 128)),
                                                       reads=[p.b, cp.b], writes=[qTg[g].b]))
            wk_ = load_w(w_in_v, ck)
            yield from proj_fm_g(wk_, 8, lambda k, c0, n: hT[:, k, kc0 + c0:kc0 + c0 + n], lambda c0, n: hbufs(kc0 + c0, n), etiles(nkc),
                                 lambda p, c0, n: S.op("act", lambda e: e.activation(kTg[g].t[:, c0:c0 + n], p.t[:, 0:n], AF.Identity, bias=cpc(CP_BIN + ck // 128)),
                                                       reads=[p.b, cp.b], writes=[kTg[g].b]))
            wv_ = load_w(w_in_v, cv)
            yield from proj_fm_g(wv_, 8, lambda k, c0, n: hT[:, k, kc0 + c0:kc0 + c0 + n], lambda c0, n: hbufs(kc0 + c0, n), etiles(nkc),
                                 lambda p, c0, n: S.op("dve", lambda e: e.tensor_scalar(vtt.t[:, c0:c0 + n], p.t[:, 0:n], cpc(CP_BIN + cv // 128), None, ALU.add),
                                                       reads=[p.b, cp.b], writes=[vtt.b]))
            for (gg, dd, r, ia, ib, tiles) in classes:
                for t0_ in range(0, len(tiles), 8):
                    grp = tiles[t0_:t0_ + 8]
                    pt = PT[ptr[0] % 2]
                    ptr[0] += 1
                    for si, (ks, ke, vc) in enumerate(grp):
                        M = ke - ks
                        c0 = r + d * ks + 1 + 64 * d
                        S.op("pe", lambda e, si=si, M=M, c0=c0, pt=pt: e.transpose(pt.t[0:M, si, :], vtt.t[:, c0:c0 + d * (M - 1) + 1:d], ident.t[:, :]),
                             reads=[vtt.b, ident.b], writes=[pt.b])
                    nfull = sum(1 for (ks, ke, vc) in grp if ke - ks == 128)
                    vl0 = grp[0][2] - VC0[g]
                    if nfull:
                        S.op("act", lambda e, nfull=nfull, vl0=vl0, pt=pt: e.activation(vTg[g].t[:, vl0:vl0 + nfull, :], pt.t[:, 0:nfull, :], AF.Identity),
                             reads=[pt.b], writes=[vTg[g].b])
                    for si, (ks, ke, vc) in enumerate(grp):
                        M = ke - ks
                        if M < 128:
                            S.op("act", lambda e, si=si, M=M, vl=vc - VC0[g], pt=pt: e.activation(vTg[g].t[0:M, vl, :], pt.t[0:M, si, :], AF.Identity),
                                 reads=[pt.b], writes=[vTg[g].b])
                    yield

        def gen_att(pair, g, dry=False):
            win, d = GROUPS[g]
            classes = [c for c in PLAN if c[0] == g]
            if dry:
                for c in classes:
                    for _ in range(len(c[5]) + 1):
                        yield
                return
            qT_, kT_, vT_, Wm_ = qTg[g], kTg[g], vTg[g], Wmg[g]
            for hh in range(2):
                sl = SLOPES[g * 8 + pair * 2 + hh] * d
                S.op("act", lambda e, hh=hh, sl=sl: e.activation(Wm_.t[:, hh, :], cpt[:, CP_D:CP_D + 256], AF.Exp, scale=-sl),
                     reads=[cp.b], writes=[Wm_.b])
                S.op("pool", lambda e, hh=hh: e.tensor_tensor(Wm_.t[:, hh, :], Wm_.t[:, hh, :], cpt[:, CP_M:CP_M + 256], ALU.mult),
                     reads=[cp.b, Wm_.b], writes=[Wm_.b])
            for (gg, dd, r, ia, ib, tiles) in classes:
                ntile = len(tiles)
                nseg = ntile - 1
                ptl = {}
                cur = {}
                for j in range(ntile + 1):
                    if j < ntile:
                        ks, ke, vc = tiles[j]
                        M = ke - ks
                        qlo = max(ia, ks - 64)
                        qhi = min(ib, ke + 64)
                        n = qhi - qlo
                        wo = qlo - (ks - 64)
                        kc = r + d * ks + 1 + 64 * d
                        qc = r + d * qlo + 1
                        pa, pb = SA[j % 2], SBk[j % 2]
                        S.op("pe", lambda e, M=M, n=n, kc=kc, qc=qc, pa=pa: e.matmul(
                            pa.t[0:M, 0:n], lhsT=kT_.t[0:64, kc:kc + d * (M - 1) + 1:d], rhs=qT_.t[0:64, qc:qc + d * (n - 1) + 1:d],
                            start=True, stop=True, tile_position=(0, 0)), reads=[kT_.b, qT_.b], writes=[pa.b])
                        S.op("pe", lambda e, M=M, n=n, kc=kc, qc=qc, pb=pb: e.matmul(
                            pb.t[0:M, 0:n], lhsT=kT_.t[64:128, kc:kc + d * (M - 1) + 1:d], rhs=qT_.t[64:128, qc:qc + d * (n - 1) + 1:d],
                            start=True, stop=True, tile_position=(64, 0)), reads=[kT_.b, qT_.b], writes=[pb.b])
                        pts = []
                        for hh, pp in enumerate((pa, pb)):
                            E = Ebuf[erot[0] % 4]
                            erot[0] += 1
                            Pt = Pbuf[prot[0] % 8]
                            prot[0] += 1
                            S.op("act", lambda e, E=E, pp=pp, M=M, n=n: e.activation(E.t[0:M, 0:n], pp.t[0:M, 0:n], AF.Exp, scale=0.125),
                                 reads=[pp.b], writes=[E.b])
                            S.op("dve", lambda e, E=E, Pt=Pt, M=M, n=n, vc=vc, wo=wo, hh=hh: e.scalar_tensor_tensor(
                                Pt.t[0:M, wo:wo + n], E.t[0:M, 0:n], cpt[0:M, CP_VAL + vc:CP_VAL + vc + 1], Wm_.t[0:M, hh, wo:wo + n],
                                ALU.mult, ALU.mult), reads=[E.b, cp.b, Wm_.b], writes=[Pt.b])
                            pts.append(Pt)
                        ptl[j] = (pts, M, vc - VC0[g])
                    m = j - 2
                    if 0 <= m < nseg:
                        if m % 2 == 0:
                            cur["pn"] = PNb[gctr[0] % len(PNb)]
                            cur["pd"] = PDb[gctr[0] % len(PDb)]
                            gctr[0] += 1
                            cur["m0"] = m
                        pbk, pdk = cur["pn"], cur["pd"]
                        qa = ia + 128 * m
                        ns = min(128, ib - qa)
                        pc = (m % 2) * 128
                        for ci, (tj, co) in enumerate(((m, 128), (m + 1, 0))):
                            ptsj, Mj, vlj = ptl[tj]
                            for hh in range(2):
                                S.op("pe", lambda e, hh=hh, Mj=Mj, vlj=vlj, co=co, ns=ns, pc=pc, ci=ci, Ptt=ptsj[hh], pbk=pbk: e.matmul(
                                    pbk.t[hh * 64:(hh + 1) * 64, pc:pc + ns], lhsT=vT_.t[0:Mj, vlj, hh * 64:(hh + 1) * 64], rhs=Ptt.t[0:Mj, co:co + ns],
                                    start=(ci == 0), stop=(ci == 1), tile_position=(0, hh * 64)),
                                    reads=[vT_.b, ptsj[hh].b], writes=[pbk.b])
                            for hh in range(2):
                                S.op("pe", lambda e, hh=hh, Mj=Mj, co=co, ns=ns, pc=pc, ci=ci, Ptt=ptsj[hh], pdk=pdk: e.matmul(
                                    pdk.t[hh * 64:(hh + 1) * 64, pc:pc + ns], lhsT=ones.t[0:Mj, 0:64], rhs=Ptt.t[0:Mj, co:co + ns],
                                    start=(ci == 0), stop=(ci == 1), tile_position=(0, hh * 64)),
                                    reads=[ones.b, ptsj[hh].b], writes=[pdk.b])
                        if m % 2 == 1 or m == nseg - 1:
                            m0 = cur["m0"]
                            qa0 = ia + 128 * m0
                            nq = min(ib, ia + 128 * (m + 1)) - qa0
                            e0 = r + d * qa0 + 1
                            sl_ = slice(e0, e0 + d * (nq - 1) + 1, d)
                            if g == 0:
                                S.op("dve", lambda e, sl_=sl_, nq=nq, pbk=pbk: e.tensor_copy(accn.t[:, sl_], pbk.t[:, 0:nq]), reads=[pbk.b], writes=[accn.b])
                                S.op("act", lambda e, sl_=sl_, nq=nq, pdk=pdk: e.activation(accd.t[:, sl_], pdk.t[:, 0:nq], AF.Identity), reads=[pdk.b], writes=[accd.b])
                            else:
                                S.op("dve", lambda e, sl_=sl_, nq=nq, pbk=pbk: e.tensor_tensor(accn.t[:, sl_], accn.t[:, sl_], pbk.t[:, 0:nq], ALU.add),
                                     reads=[pbk.b, accn.b], writes=[accn.b])
                                S.op("dve", lambda e, sl_=sl_, nq=nq, pdk=pdk: e.tensor_tensor(accd.t[:, sl_], accd.t[:, sl_], pdk.t[:, 0:nq], ALU.add),
                                     reads=[pdk.b, accd.b], writes=[accd.b])
                    yield
            if g == 2:
                S.op("dve", lambda e: e.reciprocal(accd.t[:, :], accd.t[:, :]), reads=[accd.b], writes=[accd.b])
                S.op("dve", lambda e: e.tensor_tensor(comb.t[:, pair, :], accn.t[:, :], accd.t[:, :], ALU.mult),
                     reads=[accn.b, accd.b], writes=[comb.b])

        def merge(streams):
            streams = [(g_, n_) for (g_, n_) in streams if n_ > 0]
            if not streams:
                return
            tot = max(n_ for _, n_ in streams)
            acc = [0.0] * len(streams)
            for _ in range(tot):
                for i_, (g_, n_) in enumerate(streams):
                    acc[i_] += n_ / tot
                    while acc[i_] >= 1.0 - 1e-9:
                        next(g_, None)
                        acc[i_] -= 1.0
            for g_, _ in streams:
                for _ in g_:
                    pass

        def take(g_, n_):
            for _ in range(n_):
                next(g_, None)
                yield

        def count(gen):
            return sum(1 for _ in gen)

        units = [(p_, g_) for p_ in range(4) for g_ in range(3)]
        tail_blocks = [6, 25, 26, 5, 27, 4, 28, 3, 29, 2, 30, 1, 31, 0, 32]
        tail = gen_p1(tail_blocks, p1b_xin, p1b_xnb)
        tail_n = len(tail_blocks) + 3
        prev = None
        for ui, u in enumerate(units + [None]):
            streams = []
            if u is not None:
                streams.append((gen_proj(u[0], u[1]), count(gen_proj(u[0], u[1], dry=True))))
            if prev is not None:
                streams.append((gen_att(prev[0], prev[1]), count(gen_att(prev[0], prev[1], dry=True))))
            if ui == 0:
                streams.append((take(tail, 8), 8))
            if ui == 1:
                streams.append((take(tail, tail_n - 8), tail_n - 8))
            if _MODE == 1:
                for g_, _n in streams[::-1]:
                    for _ in g_:
                        pass
            else:
                merge(streams)
            if ui == 1:
                for _ in tail:
                    pass
                for t_ in p1b_xin + p1b_xnb:
                    for (e_, tk_) in ([t_.b.w] if t_.b.w else []) + list(t_.b.r.items()):
                        if comb.b.r.get(e_, 0) < tk_:
                            comb.b.r[e_] = tk_
            prev = u

        dump(S, "d_comb", comb)
        if debug:
            S.dma("sp", "dbg", dbg["d_hT"][:], regA.t[:], reads=hTb)
        S.barrier()
        gen_banks[:] = [0, 1, 2, 3, 4, 5]
        NCH = NE + 2
        RR.reset()
        RL.reset()
        yap = RR.sb("yap", [128, 8, NE], BF16)
        Csb = RR.sb("Csb", [128, NCH])
        Hsb = RR.sb("Hsb", [128, NCH])
        Asb = RR.sb("Asb", [128, NE])
        for i in range(8):
            wc = load_w(w_in_v, OFF_C + i * 128)
            wh = load_w(w_in_v, OFF_H + i * 128)
            wbb = load_w(w_in_v, OFF_B + i * 128)
            for (wt_, dstT, bcol) in ((wc, Csb, OFF_C // 128 + i), (wh, Hsb, OFF_H // 128 + i)):
                proj_fm(wt_, 8, lambda k, c0, n: hT[:, k, 1023 + c0:1023 + c0 + n], lambda c0, n: hbufs(1023 + c0, n), etiles(NCH),
                        lambda p, c0, n, dstT=dstT, bcol=bcol: S.op("act", lambda e: e.activation(dstT.t[:, c0:c0 + n], p.t[:, 0:n], AF.Identity, bias=cpc(CP_BIN + bcol)),
                                                                   reads=[p.b, cp.b], writes=[dstT.b]))
            S.op("dve", lambda e: e.tensor_tensor(Csb.t[:, :], Csb.t[:, :], Hsb.t[:, :], ALU.mult), reads=[Csb.b, Hsb.b], writes=[Csb.b])
            S.op("dve", lambda e: e.tensor_scalar(Csb.t[:, 1:2], Csb.t[:, 1:2], cpc(CP_FLAG), None, ALU.mult), reads=[Csb.b, cp.b], writes=[Csb.b])
            S.op("dve", lambda e: e.tensor_scalar(Csb.t[:, NE:NE + 1], Csb.t[:, NE:NE + 1], cpc(CP_FLAG + 1), None, ALU.mult), reads=[Csb.b, cp.b], writes=[Csb.b])
            S.op("dve", lambda e, i=i: e.tensor_scalar(Asb.t[:, :], Csb.t[:, 1:NE + 1], cpc(CP_CW + 8 + i), None, ALU.mult), reads=[Csb.b, cp.b], writes=[Asb.b])
            S.op("dve", lambda e, i=i: e.scalar_tensor_tensor(Asb.t[:, :], Csb.t[:, 0:NE], cpc(CP_CW + i), Asb.t[:, :], ALU.mult, ALU.add),
                 reads=[Csb.b, cp.b, Asb.b], writes=[Asb.b])
            S.op("dve", lambda e, i=i: e.scalar_tensor_tensor(Asb.t[:, :], Csb.t[:, 2:NE + 2], cpc(CP_CW + 16 + i), Asb.t[:, :], ALU.mult, ALU.add),
                 reads=[Csb.b, cp.b, Asb.b], writes=[Asb.b])
            proj_fm(wbb, 8, lambda k, c0, n: hT[:, k, 1024 + c0:1024 + c0 + n], lambda c0, n: hbufs(1024 + c0, n), etiles(NE),
                    lambda p, c0, n, i=i: S.op("dve", lambda e: e.scalar_tensor_tensor(yap.t[:, i, c0:c0 + n], p.t[:, 0:n], cpc(CP_BIN + OFF_B // 128 + i), Asb.t[:, c0:c0 + n], ALU.add, ALU.mult),
                                               reads=[p.b, cp.b, Asb.b], writes=[yap.b]))

        dump(S, "d_yap", yap)
        S.barrier()
        RR.cur = RR.lo + 32800
        RL.reset()
        mg = RR.sb("merged", [128, 8, NE], BF16)
        sga = [RL.sb("sga%d" % i, [128, 512]) for i in range(2)]
        sgb = [RL.sb("sgb%d" % i, [128, 512]) for i in range(2)]
        tmg = [RL.sb("tmg%d" % i, [128, 512]) for i in range(2)]
        tmh = [RL.sb("tmh%d" % i, [128, 512]) for i in range(2)]
        it = [0]
        for i in range(8):
            wga = load_w(w_in_v, OFF_GA + i * 128)
            wgb = load_w(w_in_v, OFF_GB + i * 128)
            wa = load_w(w_a_v, i * 128)
            wbm = load_w(w_b_v, i * 128, nk=4)
            for (c0, n) in etiles(NE):
                q = it[0] % 2
                it[0] += 1
                A_, B_, t1_, t2_ = sga[q], sgb[q], tmg[q], tmh[q]
                proj_fm(wga, 8, lambda k, c0_, n_: hT[:, k, 1024 + c0_:1024 + c0_ + n_], lambda c0_, n_: hbufs(1024 + c0_, n_), [(c0, n)],
                        lambda p, c0_, n_, i=i, A_=A_: S.op("act", lambda e: e.activation(A_.t[:, 0:n_], p.t[:, 0:n_], AF.Sigmoid, bias=cpc(CP_BIN + OFF_GA // 128 + i)),
                                                            reads=[p.b, cp.b], writes=[A_.b]))
                proj_fm(wa, 8, lambda k, c0_, n_: yap.t[:, k, c0_:c0_ + n_], [yap.b], [(c0, n)],
                        lambda p, c0_, n_, A_=A_, t1_=t1_: S.op("dve", lambda e: e.tensor_tensor(t1_.t[:, 0:n_], p.t[:, 0:n_], A_.t[:, 0:n_], ALU.mult),
                                                               reads=[p.b, A_.b], writes=[t1_.b]))
                proj_fm(wgb, 8, lambda k, c0_, n_: hT[:, k, 1024 + c0_:1024 + c0_ + n_], lambda c0_, n_: hbufs(1024 + c0_, n_), [(c0, n)],
                        lambda p, c0_, n_, i=i, B_=B_: S.op("act", lambda e: e.activation(B_.t[:, 0:n_], p.t[:, 0:n_], AF.Sigmoid, bias=cpc(CP_BIN + OFF_GB // 128 + i)),
                                                            reads=[p.b, cp.b], writes=[B_.b]))
                proj_fm(wbm, 4, lambda k, c0_, n_: comb.t[:, k, c0_:c0_ + n_], [comb.b], [(c0, n)],
                        lambda p, c0_, n_, B_=B_, t2_=t2_: S.op("dve", lambda e: e.tensor_tensor(t2_.t[:, 0:n_], p.t[:, 0:n_], B_.t[:, 0:n_], ALU.mult),
                                                               reads=[p.b, B_.b], writes=[t2_.b]))
                S.op("pool", lambda e, i=i, c0=c0, n=n, t1_=t1_, t2_=t2_: e.tensor_tensor(mg.t[:, i, c0:c0 + n], t1_.t[:, 0:n], t2_.t[:, 0:n], ALU.add),
                     reads=[t1_.b, t2_.b], writes=[mg.b])

        dump(S, "d_mg", mg)
        S.barrier()
        sb = R2.sb
        wo_k = [sb("wo_sb%d" % k, [128, D], BF16) for k in range(8)]
        for k in range(8):
            S.dma("pool", "w", wo_k[k].t[:, :], w_o_v[:, k, :], writes=[wo_k[k].b])
        bcB = sb("bcB", [128, 3, D])
        S.dma("sp", "c", bcB.t[:, 0, :], bc["b_o"][:], writes=[bcB.b])
        S.dma("sp", "c", bcB.t[:, 1, :], bc["ln1_g"][:], writes=[bcB.b])
        S.dma("sp", "c", bcB.t[:, 2, :], bc["ln1_b"][:], writes=[bcB.b])
        hres = [sb("hres%d" % i, [128, D]) for i in range(4)]
        zt = [sb("zt%d" % i, [128, D]) for i in range(3)]
        znt = [sb("znt%d" % i, [128, D]) for i in range(3)]
        tmp1 = [sb("tmp1_%d" % i, [128, D]) for i in range(2)]
        hb = [sb("hb%d" % i, [128, D], BF16) for i in range(3)]
        h1T = at("h1T", [128, 8, NE], BF16, RR.lo)
        hs_b = Buf("h_scr")
        p5ps = {}

        def blkM(b):
            return 128 if b < 16 else 2

        def p5_s0(b):
            M = blkM(b)
            S.dma("sp", "hl", hres[b % 4].t[0:M, :], h_scr[b * 128:b * 128 + M, :], reads=[hs_b], writes=[hres[b % 4].b])

        def p5_s1(b):
            M, e0 = blkM(b), b * 128
            p5ps[b] = []
            for half in range(2):
                p = getps()
                p5ps[b].append(p)
                for k in range(8):
                    S.op("pe", lambda e, k=k, p=p, half=half: e.matmul(p.t[0:M, :], lhsT=mg.t[:, k, e0:e0 + M], rhs=wo_k[k].t[:, half * 512:(half + 1) * 512],
                                                                   start=(k == 0), stop=(k == 7)), reads=[mg.b, wo_k[k].b], writes=[p.b])

        def p5_s2(b):
            M = blkM(b)
            hr, z = hres[b % 4], zt[b % 3]
            for half in range(2):
                p = p5ps[b][half]
                S.op("dve", lambda e, p=p, half=half: e.tensor_tensor(z.t[0:M, half * 512:(half + 1) * 512], hr.t[0:M, half * 512:(half + 1) * 512], p.t[0:M, :], ALU.add),
                     reads=[p.b, hr.b], writes=[z.b])
            ln_stats_a(z, M, b)

        def p5_s3(b):
            M = blkM(b)
            z, zn = zt[b % 3], znt[b % 3]
            ln_stats_b(z, zn, M, b)
            S.dma("act", "zs", zn_scr[b * 128:b * 128 + M, :], zn.t[0:M, :], reads=[zn.b])

        def p5_s4(b):
            M = blkM(b)
            zn, t1, h = znt[b % 3], tmp1[b % 2], hb[b % 3]
            S.op("pool", lambda e: e.tensor_tensor(t1.t[0:M, :], zn.t[0:M, :], bcB.t[0:M, 1, :], ALU.mult), reads=[zn.b, bcB.b], writes=[t1.b])
            S.op("pool", lambda e: e.tensor_tensor(h.t[0:M, :], t1.t[0:M, :], bcB.t[0:M, 2, :], ALU.add), reads=[t1.b, bcB.b], writes=[h.b])

        def p5_s5(b):
            M = blkM(b)
            transpose_to(hb[b % 3], M, h1T.t[:, :, b * 128:b * 128 + M], h1T.b, b)

        run_pipeline(17, [p5_s0, p5_s1, p5_s2, p5_s3, p5_s4, p5_s5])
        dump(S, "d_h1T", h1T)
        S.dma("sp", "c", bcA.t[:, 3, :], bc["b_down"][:], writes=[bcA.b])
        S.op("dve", lambda e: e.scalar_tensor_tensor(bcA.t[:, 3, :], bcB.t[:, 2, :], ALPHA, bcA.t[:, 3, :], ALU.mult, ALU.add), reads=[bcA.b, bcB.b], writes=[bcA.b])
        S.op("dve", lambda e: e.tensor_scalar(bcA.t[:, 2, :], bcB.t[:, 1, :], ALPHA, None, ALU.mult), reads=[bcA.b, bcB.b], writes=[bcA.b])

        S.barrier()
        zs_b = Buf("zn_scr")
        S.dma("sp", "c", bcA.t[:, 0, :], bc["ln2_g"][:], writes=[bcA.b])
        S.dma("sp", "c", bcA.t[:, 1, :], bc["ln2_b"][:], writes=[bcA.b])
        fT = at("fT", [128, 22, 1024], BF16, RR.lo + 32800)
        R2.lo = R2.cur = RR.lo + 32800 + 45056
        sb = R2.sb
        wd_k = [sb("wd_sb%d" % k, [128, D], BF16) for k in range(22)]
        hres = [sb("hres%d" % i, [128, D]) for i in range(3)]
        zt = [sb("zt%d" % i, [128, D]) for i in range(3)]
        znt = [sb("znt%d" % i, [128, D]) for i in range(2)]
        NA = 1026
        asb = [sb("asb%d" % i, [128, NA]) for i in range(2)]
        csb = [sb("csb%d" % i, [128, 1024]) for i in range(2)]
        for half in range(2):
            eb = half * 1024
            for i in range(22):
                wa_ = load_w(w_up_v, i * 128)
                wg_ = load_w(w_up_v, DFF + i * 128)
                if half == 0 and i >= 2:
                    for k_ in ((i - 2, i - 1) if i == 21 else (i - 2,)) if i < 21 else (19, 20, 21):
                        S.dma("pool", "w", wd_k[k_].t[:, :], w_down_v[:, k_, :], writes=[wd_k[k_].b])
                a_, c_ = asb[i % 2], csb[i % 2]
                g_ = c_
                proj_fm(wa_, 8, lambda k, c0, n, eb=eb: h1T.t[:, k, eb + c0:eb + c0 + n], [h1T.b], etiles(NA),
                        lambda p, c0, n, a_=a_, i=i: S.op("act", lambda e: e.activation(a_.t[:, c0:c0 + n], p.t[:, 0:n], AF.Identity, bias=cpc(CP_BUP + i)),
                                                          reads=[p.b, cp.b], writes=[a_.b]))
                if half == 0:
                    S.op("dve", lambda e, a_=a_: e.tensor_scalar(a_.t[:, 0:1], a_.t[:, 0:1], cpc(CP_FLAG), None, ALU.mult), reads=[a_.b, cp.b], writes=[a_.b])
                else:
                    S.op("dve", lambda e, a_=a_: e.tensor_scalar(a_.t[:, NA - 1:NA], a_.t[:, NA - 1:NA], cpc(CP_FLAG + 1), None, ALU.mult), reads=[a_.b, cp.b], writes=[a_.b])
                S.op("dve", lambda e, a_=a_, c_=c_, i=i: e.tensor_scalar(c_.t[:, :], a_.t[:, 1:1025], cpc(CP_FCW + 22 + i), cpc(CP_FCB + i), ALU.mult, ALU.add),
                     reads=[a_.b, cp.b], writes=[c_.b])
                S.op("dve", lambda e, a_=a_, c_=c_, i=i: e.scalar_tensor_tensor(c_.t[:, :], a_.t[:, 0:1024], cpc(CP_FCW + i), c_.t[:, :], ALU.mult, ALU.add),
                     reads=[a_.b, cp.b, c_.b], writes=[c_.b])
                S.op("dve", lambda e, a_=a_, c_=c_, i=i: e.scalar_tensor_tensor(c_.t[:, :], a_.t[:, 2:1026], cpc(CP_FCW + 44 + i), c_.t[:, :], ALU.mult, ALU.add),
                     reads=[a_.b, cp.b, c_.b], writes=[c_.b])
                S.op("act", lambda e, c_=c_: e.activation(c_.t[:, :], c_.t[:, :], AF.Gelu), reads=[c_.b], writes=[c_.b])
                proj_fm(wg_, 8, lambda k, c0, n, eb=eb: h1T.t[:, k, eb + 1 + c0:eb + 1 + c0 + n], [h1T.b], etiles(1024),
                        lambda p, c0, n, g_=g_, i=i: S.op("dve", lambda e: e.scalar_tensor_tensor(fT.t[:, i, c0:c0 + n], p.t[:, 0:n], cpc(CP_BUP + 22 + i), g_.t[:, c0:c0 + n], ALU.add, ALU.mult),
                                                          reads=[p.b, cp.b, g_.b], writes=[fT.b]))
            p6ps = {}

            def p6_s0(tb, eb=eb):
                erow = eb + 1 + tb * 128
                S.dma("sp", "zl", hres[tb % 3].t[:, :], zn_scr[erow:erow + 128, :], reads=[zs_b], writes=[hres[tb % 3].b])

            def p6_s1(tb):
                zr, t0 = hres[tb % 3], tb * 128
                S.op("pool", lambda e: e.tensor_tensor(zr.t[:, :], zr.t[:, :], bcA.t[:, 2, :], ALU.mult), reads=[zr.b, bcA.b], writes=[zr.b])
                S.op("pool", lambda e: e.tensor_tensor(zr.t[:, :], zr.t[:, :], bcA.t[:, 3, :], ALU.add), reads=[zr.b, bcA.b], writes=[zr.b])
                p6ps[tb] = []
                for hf_ in range(2):
                    p = getps()
                    p6ps[tb].append(p)
                    for k in range(22):
                        S.op("pe", lambda e, k=k, p=p, hf_=hf_: e.matmul(p.t[:, :], lhsT=fT.t[:, k, t0:t0 + 128], rhs=wd_k[k].t[:, hf_ * 512:(hf_ + 1) * 512],
                                                                     start=(k == 0), stop=(k == 21)), reads=[fT.b, wd_k[k].b], writes=[p.b])

            def p6_s2(tb):
                zr, z = hres[tb % 3], zt[tb % 3]
                for hf_ in range(2):
                    p = p6ps[tb][hf_]
                    S.op("dve", lambda e, p=p, hf_=hf_: e.tensor_tensor(z.t[:, hf_ * 512:(hf_ + 1) * 512], zr.t[:, hf_ * 512:(hf_ + 1) * 512], p.t[:, :], ALU.add),
                         reads=[p.b, zr.b], writes=[z.b])
                ln_stats_a(z, 128, tb)

            def p6_s3(tb):
                ln_stats_b(zt[tb % 3], znt[tb % 2], 128, tb)

            def p6_s4(tb, half=half):
                zn, o_ = znt[tb % 2], zt[tb % 3]
                S.op("dve", lambda e: e.tensor_tensor(zn.t[:, :], zn.t[:, :], bcA.t[:, 0, :], ALU.mult), reads=[zn.b, bcA.b], writes=[zn.b])
                S.op("pool", lambda e: e.tensor_tensor(o_.t[:, :], zn.t[:, :], bcA.t[:, 1, :], ALU.add), reads=[zn.b, bcA.b], writes=[o_.b])
                r0 = half * 1024 + tb * 128
                S.dma("pool", "o", out[r0:r0 + 128, :], o_.t[:, :], reads=[o_.b])

            run_pipeline(8, [p6_s0, p6_s1, p6_s2, p6_s3, p6_s4])
        for nm_, och in S.chans.items():
            if nm_.startswith("o_"):
                S.prog["sp"].append(("wait", och.sem, och.count))
        S.emit()
    return nc


_CACHE = {}


def _host_consts():
    D_ = np.zeros((128, 256), np.float32)
    M_ = np.zeros((128, 256), np.float32)
    for k in range(128):
        for q in range(256):
            dist = abs(q - 64 - k)
            D_[k, q] = min(dist, 64)
            M_[k, q] = 1.0 if dist <= 64 else 0.0
    return D_, M_


def kernel(x, ln0_g, ln0_b, w_in, b_in, conv_w, w_a, w_b, w_o, b_o, ln1_g, ln1_b,
           w_up, b_up, ffn_conv_w, ffn_conv_b, w_down, b_down, ln2_g, ln2_b):
    f = lambda a: np.ascontiguousarray(np.asarray(a, dtype=np.float32))
    x = f(x)
    if "nc" not in _CACHE:
        _CACHE["nc"] = build_program(_CACHE.get("debug", False))
    nc = _CACHE["nc"]
    attD, attM = _host_consts()
    bcast = lambda v: np.ascontiguousarray(np.broadcast_to(f(v).reshape(1, -1), (128, f(v).size)))
    common = {
        "w_in": f(w_in)[0], "w_a": f(w_a)[0], "w_b": f(w_b)[0], "w_o": f(w_o)[0], "w_up": f(w_up)[0], "w_down": f(w_down)[0],
        "ident": np.eye(128, dtype=np.float32),
        "bc_ln0_g": bcast(ln0_g), "bc_ln0_b": bcast(ln0_b), "bc_b_v": bcast(f(b_in)[0, OFF_V:OFF_V + 1536]),
        "bc_b_o": bcast(b_o), "bc_ln1_g": bcast(ln1_g), "bc_ln1_b": bcast(ln1_b), "bc_b_down": bcast(b_down),
        "bc_ln2_g": bcast(ln2_g), "bc_ln2_b": bcast(ln2_b),
    }
    cp0 = np.zeros((128, NCP), np.float32)
    cp0[:, CP_BIN:CP_BIN + 76] = f(b_in)[0].reshape(76, 128).T
    cp0[:, CP_CW:CP_CW + 24] = f(conv_w)[0].reshape(3, 8, 128).transpose(2, 0, 1).reshape(128, 24)
    cp0[:, CP_BUP:CP_BUP + 44] = f(b_up)[0].reshape(44, 128).T
    cp0[:, CP_FCW:CP_FCW + 66] = f(ffn_conv_w)[0].reshape(3, 22, 128).transpose(2, 0, 1).reshape(128, 66)
    cp0[:, CP_FCB:CP_FCB + 22] = f(ffn_conv_b)[0].reshape(22, 128).T
    cp0[:, CP_D:CP_D + 256] = attD
    cp0[:, CP_M:CP_M + 256] = attM
    cp0[:, CP_EPS] = LN_EPS
    cp0[:, CP_G0:CP_G0 + 8] = f(ln0_g).reshape(8, 128).T
    cp0[:, CP_B0:CP_B0 + 8] = f(ln0_b).reshape(8, 128).T
    in_maps = []
    for c in range(NCORE):
        b, s0 = c // 4, (c % 4) * TPC
        xe = np.zeros((NX, D), np.float32)
        lo, hi = s0 - HL, s0 - HL + NX
        a, bb = max(lo, 0), min(hi, SEQ)
        xe[a - lo:bb - lo] = x[b, a:bb]
        cpc_ = cp0.copy()
        cpc_[:, CP_FLAG] = 1.0 if s0 > 0 else 0.0
        cpc_[:, CP_FLAG + 1] = 1.0 if s0 + TPC < SEQ else 0.0
        for (g, d, r, ia, ib, tiles) in PLAN:
            for (ks, ke, vc) in tiles:
                tok = s0 + r + d * (ks + np.arange(128))
                cpc_[:, CP_VAL + vc] = ((tok >= 0) & (tok < SEQ) & (np.arange(128) < ke - ks)).astype(np.float32)
        m = dict(common)
        m["x_ext"] = xe
        m["cpart"] = cpc_
        in_maps.append(m)
    res = run_bass_kernel_spmd(nc, in_maps, core_ids=list(range(NCORE)))
    _CACHE["res"] = res
    outp = np.zeros((2, SEQ, D), np.float32)
    for c in range(NCORE):
        b, s0 = c // 4, (c % 4) * TPC
        outp[b, s0:s0 + TPC] = res.results[c]["out"]
    return outp
```

```python
import numpy as np
from contextlib import ExitStack
import concourse.bass as bass
import concourse.mybir as mybir
from concourse.bass_utils import run_bass_kernel_spmd

F32 = mybir.dt.float32
BF16 = mybir.dt.bfloat16
AF = mybir.ActivationFunctionType
ALU = mybir.AluOpType

D = 1024
NIN = 9728
DFF = 2816
SEQ = 8192
NCORE = 8
TPC = 2048
NE = 2050
HL = 1025
NXB = 33
NX = NXB * 128
OFF_B, OFF_C, OFF_H, OFF_Q, OFF_K, OFF_V, OFF_GA, OFF_GB = 0, 1024, 2048, 3072, 4608, 6144, 7680, 8704
GROUPS = ((128, 1), (512, 4), (2048, 16))
ALPHA = 2.0 ** 0.25
LN_EPS = 1e-5
SLOPES = [float(2.0 ** (-8.0 * (i + 1) / 24.0)) for i in range(24)]

CP_BIN = 0
CP_CW = 76
CP_BUP = 100
CP_FCW = 144
CP_FCB = 210
CP_FLAG = 232
CP_VAL = 234


def class_plan():
    plan = []
    vcol = 0
    for g, (win, d) in enumerate(GROUPS):
        for r in range(d):
            ia = -((1 + r) // d)
            ib = (2048 - r) // d + 1
            ntile = (ib - ia + 127) // 128 + 1
            tiles = []
            for j in range(ntile):
                ks = ia - 64 + 128 * j
                ke = min(ks + 128, ib + 64)
                tiles.append((ks, ke, vcol))
                vcol += 1
            plan.append((g, d, r, ia, ib, tiles))
    return plan, vcol


import os as _os
_MODE = int(_os.environ.get('MK_MODE', '0'))
PLAN, NVT = class_plan()
SB_BASE = 16512
SB_LIMIT = 229376
NCP = CP_VAL + NVT + 512 + 1 + 16
CP_D = CP_VAL + NVT
CP_M = CP_D + 256
CP_EPS = CP_M + 256
CP_G0 = CP_EPS + 1
CP_B0 = CP_G0 + 8


class Buf:
    __slots__ = ("name", "w", "r")

    def __init__(self, name):
        self.name = name
        self.w = None
        self.r = {}


class Eng:
    def __init__(self, name, sem):
        self.name = name
        self.sem = sem
        self.count = 0
        self.seen = {}


class Sched:
    def __init__(self, nc, stack):
        self.nc = nc
        self.stack = stack
        self.engs = {}
        for nm in ("pe", "act", "dve", "pool", "sp"):
            sem = stack.enter_context(nc.semaphore("s_" + nm))
            self.engs[nm] = Eng(nm, sem)
        self.chans = {}
        self.prog = {nm: [] for nm in self.engs}

    def chan(self, name):
        if name not in self.chans:
            sem = self.stack.enter_context(self.nc.semaphore("d_" + name))
            self.chans[name] = Eng("dma_" + name, sem)
        return self.chans[name]

    def _deps(self, eng, reads, writes):
        deps = {}
        for b in reads:
            if b.w is not None:
                e, t = b.w
                if deps.get(e, 0) < t:
                    deps[e] = t
        for b in writes:
            if b.w is not None:
                e, t = b.w
                if deps.get(e, 0) < t:
                    deps[e] = t
            for e, t in b.r.items():
                if deps.get(e, 0) < t:
                    deps[e] = t
        for e, t in deps.items():
            if e is eng and eng.name in ("pe", "sp"):
                continue
            if eng.seen.get(e, 0) < t:
                self.prog[eng.name].append(("wait", e.sem, t))
                eng.seen[e] = t

    def op(self, en, fn, reads=(), writes=()):
        eng = self.engs[en]
        self._deps(eng, reads, writes)
        eng.count += 1
        self.prog[en].append(("op", fn, eng.sem, 1))
        t = eng.count
        for b in reads:
            b.r[eng] = t
        for b in writes:
            b.w = (eng, t)
            b.r = {}

    def dma(self, qn, chan, out, in_, reads=(), writes=()):
        q = self.engs[qn]
        ch = self.chan(chan + "_" + (writes[0].name if writes else reads[0].name))
        self._deps(q, reads, writes)
        ch.count += 16
        self.prog[qn].append(("op", (lambda e, out=out, in_=in_: e.dma_start(out=out, in_=in_)), ch.sem, 16))
        t = ch.count
        for b in reads:
            b.r[ch] = t
        for b in writes:
            b.w = (ch, t)
            b.r = {}

    def barrier(self):
        allc = list(self.engs.values()) + list(self.chans.values())
        for eng in self.engs.values():
            for o in allc:
                if o is eng or o.count == 0:
                    continue
                if eng.seen.get(o, 0) < o.count:
                    self.prog[eng.name].append(("wait", o.sem, o.count))
                    eng.seen[o] = o.count

    def emit(self):
        nc = self.nc
        with nc.Block() as block:
            def runner(nm):
                def _(e):
                    for it in self.prog[nm]:
                        if it[0] == "wait":
                            e.wait_ge(it[1], it[2])
                        else:
                            it[1](e).then_inc(it[2], it[3])
                return _
            block.tensor(runner("pe"))
            block.scalar(runner("act"))
            block.vector(runner("dve"))
            block.gpsimd(runner("pool"))
            block.sync(runner("sp"))


class T:
    def __init__(self, t, name):
        self.t = t
        self.b = Buf(name)


def etiles(n, step=512):
    return [(a, min(step, n - a)) for a in range(0, n, step)]


def build_program(debug=False):
    nc = bass.Bass("TRN2", target_bir_lowering=False)
    dbg = {}

    def din(name, shape, dtype=F32):
        return nc.dram_tensor(name, list(shape), dtype, kind="ExternalInput").ap()

    x_ext = din("x_ext", [NX, D])
    w_in = din("w_in", [D, NIN])
    w_a = din("w_a", [D, D])
    w_b = din("w_b", [512, D])
    w_o = din("w_o", [D, D])
    w_up = din("w_up", [D, 2 * DFF])
    w_down = din("w_down", [DFF, D])
    cpart = din("cpart", [128, NCP])
    ident_in = din("ident", [128, 128])
    bc = {n: din("bc_" + n, [128, w]) for n, w in
          (("ln0_g", D), ("ln0_b", D), ("b_v", 1536), ("b_o", D), ("ln1_g", D), ("ln1_b", D),
           ("b_down", D), ("ln2_g", D), ("ln2_b", D))}
    out = nc.dram_tensor("out", [TPC, D], F32, kind="ExternalOutput").ap()
    zn_scr = nc.dram_tensor("zn_scr", [17 * 128, D], F32, kind="ExternalOutput" if debug else "Internal").ap()
    h_scr = nc.dram_tensor("h_scr", [17 * 128, D], F32, kind="Internal").ap()
    if debug:
        for n_, shp_ in (("d_qT", [128, 3, NE]), ("d_kT", [128, 2178 + 2562 + 4098]), ("d_vT", [128, NVT, 128])):
            dbg[n_] = nc.dram_tensor(n_, shp_, BF16, kind="ExternalOutput").ap()
        for n_, shp_ in (("d_accn", [128, NE]), ("d_accd", [128, NE]), ("d_Wm", [128, 6, 256])):
            dbg[n_] = nc.dram_tensor(n_, shp_, F32, kind="ExternalOutput").ap()
        for n_, shp_ in (("d_hT", [128, 8, NX]), ("d_comb", [128, 4, NE]), ("d_yap", [128, 8, NE]), ("d_mg", [128, 8, NE]), ("d_h1T", [128, 8, NE])):
            dbg[n_] = nc.dram_tensor(n_, shp_, BF16, kind="ExternalOutput").ap()

    def dump(S, name, t):
        if debug and name in dbg:
            S.dma("sp", "dbg", dbg[name][:], t.t[:], reads=[t.b])

    w_in_v = w_in.rearrange("(k p) n -> p k n", p=128)
    w_a_v = w_a.rearrange("(k p) n -> p k n", p=128)
    w_b_v = w_b.rearrange("(k p) n -> p k n", p=128)
    w_o_v = w_o.rearrange("(k p) n -> p k n", p=128)
    w_up_v = w_up.rearrange("(k p) n -> p k n", p=128)
    w_down_v = w_down.rearrange("(k p) n -> p k n", p=128)

    with ExitStack() as st:
        S = Sched(nc, st)

        uid = [0]

        def at(name, shape, dtype, off):
            uid[0] += 1
            nb = int(np.prod(shape[1:])) * (2 if dtype == BF16 else 4)
            assert off % 32 == 0 and off + nb <= SB_LIMIT, (name, off, nb)
            return T(nc.alloc_sbuf_tensor_at("%s_%d" % (name, uid[0]), list(shape), dtype, offset=off), name)

        class Region:
            def __init__(self, lo, hi):
                self.lo, self.hi, self.cur = lo, hi, lo

            def reset(self):
                self.cur = self.lo

            def sb(self, name, shape, dtype=F32):
                nb = int(np.prod(shape[1:])) * (2 if dtype == BF16 else 4)
                nb = (nb + 31) // 32 * 32
                assert self.cur + nb <= self.hi, (name, self.cur, nb, self.hi)
                t = at(name, shape, dtype, self.cur)
                self.cur += nb
                return t

        PB_ = SB_BASE + 37376
        PERS = Region(SB_BASE, PB_)
        RR = Region(PB_, PB_ + 67584)
        RA = Region(PB_ + 67584, PB_ + 135168)
        RC = Region(PB_ + 135168, PB_ + 151584)
        RL = Region(PB_ + 151584, SB_LIMIT)
        R2 = Region(PB_ + 67584, SB_LIMIT)
        sb = PERS.sb

        def ps(name, shape, dtype=F32):
            return T(st.enter_context(nc.psum_tensor(name, list(shape), dtype)), name)

        regA = RA.sb("regA", [128, 8, NX], BF16)
        cp = sb("cp", [128, NCP])
        ident = sb("ident", [128, 128], BF16)
        ones = sb("ones", [128, 64], BF16)
        bcA = sb("bcA", [128, 4, D])
        wbuf = [sb("wbuf%d" % i, [128, 8, 128], BF16) for i in range(8)]
        ident_f = RL.sb("ident_f", [128, 128])
        NST = 4
        st6 = [sb("st6_%d" % i, [128, 2, 6]) for i in range(NST)]
        mv = [sb("mv%d" % i, [128, 2]) for i in range(NST)]
        rs = [sb("rs%d" % i, [128, 1]) for i in range(NST)]
        nm = [sb("nm%d" % i, [128, 1]) for i in range(NST)]
        P = [ps("ps%d" % i, [128, 512]) for i in range(8)]

        class _PTv:
            def __init__(self, p):
                self.t = p.t[:, :].bitcast(BF16).rearrange("p (a b) -> p a b", a=8)
                self.b = p.b

        PT = [_PTv(P[6]), _PTv(P[7])]

        hT = regA.t
        S.dma("sp", "c", cp.t[:], cpart[:], writes=[cp.b])
        S.dma("sp", "c", ident_f.t[:], ident_in[:], writes=[ident_f.b])
        S.op("dve", lambda e: e.tensor_copy(ident.t[:], ident_f.t[:]), reads=[ident_f.b], writes=[ident.b])
        S.op("pool", lambda e: e.memset(ones.t[:], 1.0), writes=[ones.b])
        S.dma("sp", "c", bcA.t[:, 0, :], bc["ln0_g"][:], writes=[bcA.b])
        S.dma("sp", "c", bcA.t[:, 1, :], bc["ln0_b"][:], writes=[bcA.b])
        S.dma("sp", "c", bcA.t[:, 3, :], bc["b_o"][:], writes=[bcA.b])
        S.op("dve", lambda e: e.tensor_scalar(bcA.t[:, 2, :], bcA.t[:, 0, :], ALPHA, None, ALU.mult), reads=[bcA.b], writes=[bcA.b])
        S.op("dve", lambda e: e.scalar_tensor_tensor(bcA.t[:, 3, :], bcA.t[:, 1, :], ALPHA, bcA.t[:, 3, :], ALU.mult, ALU.add), reads=[bcA.b], writes=[bcA.b])

        cpt = cp.t

        def cpc(col):
            return cpt[:, col:col + 1]

        WSEQ = []
        for pair_ in range(4):
            for g_ in range(3):
                for o_ in (OFF_Q, OFF_K, OFF_V):
                    WSEQ.append((w_in_v, o_ + g_ * 512 + pair_ * 128, 8))
        for i_ in range(8):
            for o_ in (OFF_C, OFF_H, OFF_B):
                WSEQ.append((w_in_v, o_ + i_ * 128, 8))
        for i_ in range(8):
            WSEQ += [(w_in_v, OFF_GA + i_ * 128, 8), (w_in_v, OFF_GB + i_ * 128, 8), (w_a_v, i_ * 128, 8), (w_b_v, i_ * 128, 4)]
        for h_ in range(2):
            for i_ in range(22):
                WSEQ += [(w_up_v, i_ * 128, 8), (w_up_v, DFF + i_ * 128, 8)]
        wst = {"issued": 0, "taken": 0}
        LOOK = 4

        def issue_w(n):
            while wst["issued"] < min(n, len(WSEQ)):
                view, c0, nk = WSEQ[wst["issued"]]
                wb = wbuf[wst["issued"] % len(wbuf)]
                S.dma("pool", "w", wb.t[:, 0:nk, :], view[:, 0:nk, c0:c0 + 128], writes=[wb.b])
                wst["issued"] += 1

        def load_w(view, c0, nk=8):
            ent = WSEQ[wst["taken"]]
            assert ent[0] is view and ent[1] == c0 and ent[2] == nk, (wst["taken"], c0, nk)
            issue_w(wst["taken"] + 1 + LOOK)
            wb = wbuf[wst["taken"] % len(wbuf)]
            wst["taken"] += 1
            return wb

        issue_w(LOOK)

        psrot = [0]
        gen_banks = [0, 1, 2, 3, 4, 5]

        def getps():
            p = P[gen_banks[psrot[0] % len(gen_banks)]]
            psrot[0] += 1
            return p

        def run_pipeline(n, stages):
            ns = len(stages)
            for t_ in range(n + ns - 1):
                for k_ in range(ns - 1, -1, -1):
                    b_ = t_ - k_
                    if 0 <= b_ < n:
                        stages[k_](b_)

        def ln_stats_a(src, M, i):
            i %= NST
            for a_ in range(2):
                S.op("dve", lambda e, a_=a_: e.bn_stats(st6[i].t[0:M, a_, :], src.t[0:M, a_ * 512:(a_ + 1) * 512]),
                     reads=[src.b], writes=[st6[i].b])
            S.op("dve", lambda e: e.bn_aggr(mv[i].t[0:M, :], st6[i].t[0:M, :, :].rearrange("p a f -> p (a f)")),
                 reads=[st6[i].b], writes=[mv[i].b])
            S.op("act", lambda e: e.activation(rs[i].t[0:M, :], mv[i].t[0:M, 1:2], AF.Sqrt, bias=cpt[0:M, CP_EPS:CP_EPS + 1], scale=1.0),
                 reads=[mv[i].b, cp.b], writes=[rs[i].b])

        def ln_stats_b(src, dst, M, i):
            i %= NST
            S.op("dve", lambda e: e.reciprocal(rs[i].t[0:M, :], rs[i].t[0:M, :]), reads=[rs[i].b], writes=[rs[i].b])
            S.op("dve", lambda e: e.scalar_tensor_tensor(nm[i].t[0:M, :], mv[i].t[0:M, 0:1], -1.0, rs[i].t[0:M, :], ALU.mult, ALU.mult),
                 reads=[mv[i].b, rs[i].b], writes=[nm[i].b])
            S.op("act", lambda e: e.activation(dst.t[0:M, :], src.t[0:M, :], AF.Identity, bias=nm[i].t[0:M, 0:1], scale=rs[i].t[0:M, 0:1]),
                 reads=[src.b, rs[i].b, nm[i].b], writes=[dst.b])

        def transpose_to(srcbf, M, dst_ap, dst_buf, i):
            pt = PT[i % 2]
            for k in range(8):
                S.op("pe", lambda e, k=k: e.transpose(pt.t[:, k, 0:M], srcbf.t[0:M, k * 128:(k + 1) * 128], ident.t[0:M, 0:M]),
                     reads=[srcbf.b, ident.b], writes=[pt.b])
            S.op("act", lambda e: e.activation(dst_ap, pt.t[:, :, 0:M], AF.Identity), reads=[pt.b], writes=[dst_buf])

        hTb = [Buf("hT%d" % i) for i in range(NXB)]

        def hbufs(c0, n):
            return hTb[c0 // 128:(c0 + n - 1) // 128 + 1]

        def gen_p1(blocks, xin, xnb, xn=None, tmp1=None, hf=None):
            nb_ = len(blocks)

            def s0(i):
                b = blocks[i]
                S.dma("sp", "x", xin[i % len(xin)].t[:, :], x_ext[b * 128:(b + 1) * 128, :], writes=[xin[i % len(xin)].b])

            def s1(i):
                ln_stats_a(xin[i % len(xin)], 128, i)

            def s2(i):
                b = blocks[i]
                ln_stats_b(xin[i % len(xin)], xnb[i % len(xnb)], 128, i)
                if 8 <= b <= 24:
                    j = i % NST
                    xi, xo = xin[i % len(xin)], xn[i % 2]
                    S.op("act", lambda e: e.activation(xo.t[:, :], xi.t[:, :], AF.Identity, bias=nm[j].t[:, 0:1], scale=rs[j].t[:, 0:1]),
                         reads=[xi.b, rs[j].b, nm[j].b], writes=[xo.b])

            def s3(i):
                b = blocks[i]
                pt = PT[i % 2]
                src = xnb[i % len(xnb)]
                for k in range(8):
                    S.op("pe", lambda e, k=k: e.transpose(pt.t[:, k, :], src.t[:, k * 128:(k + 1) * 128], ident.t[:, :]),
                         reads=[src.b, ident.b], writes=[pt.b])
                if 8 <= b <= 24:
                    t1, xo, f_ = tmp1[i % 2], xn[i % 2], hf[i % 2]
                    S.op("dve", lambda e: e.tensor_tensor(t1.t[:, :], xo.t[:, :], bcA.t[:, 2, :], ALU.mult), reads=[xo.b, bcA.b], writes=[t1.b])
                    S.op("pool", lambda e: e.tensor_tensor(f_.t[:, :], t1.t[:, :], bcA.t[:, 3, :], ALU.add), reads=[t1.b, bcA.b], writes=[f_.b])
                    S.dma("sp", "hs", h_scr[(b - 8) * 128:(b - 7) * 128, :], f_.t[:, :], reads=[f_.b])

            def s4(i):
                b = blocks[i]
                pt = PT[i % 2]
                for k in range(8):
                    dst = hT[:, k, b * 128:(b + 1) * 128]
                    if i % 2 == 0:
                        S.op("act", lambda e, k=k, dst=dst: e.activation(dst, pt.t[:, k, :], AF.Identity, bias=cpc(CP_B0 + k), scale=cpc(CP_G0 + k)),
                             reads=[pt.b, cp.b], writes=[hTb[b]])
                    else:
                        S.op("dve", lambda e, k=k, dst=dst: e.tensor_scalar(dst, pt.t[:, k, :], cpc(CP_G0 + k), cpc(CP_B0 + k), ALU.mult, ALU.add),
                             reads=[pt.b, cp.b], writes=[hTb[b]])

            def s34(i):
                s3(i)
                s4(i)

            stages = [s0, s1, s2, s34]
            for t_ in range(nb_ + len(stages) - 1):
                for k_ in range(len(stages) - 1, -1, -1):
                    i_ = t_ - k_
                    if 0 <= i_ < nb_:
                        stages[k_](i_)
                yield

        p1a_xin = [RR.sb("xin%d" % i, [128, D]) for i in range(4)]
        p1a_xnb = [RR.sb("xnb%d" % i, [128, D], BF16) for i in range(3)]
        p1a_xn = [RR.sb("xn%d" % i, [128, D]) for i in range(2)]
        p1a_t1 = [RR.sb("tmp1_%d" % i, [128, D]) for i in range(2)]
        p1a_hf = [RR.sb("hf%d" % i, [128, D]) for i in range(2)]
        for _ in gen_p1(list(range(7, 25)), p1a_xin, p1a_xnb, p1a_xn, p1a_t1, p1a_hf):
            pass

        def proj_fm(wb, nk, rhs_fn, rhs_bufs, tiles, evac, step=None):
            for (c0, n) in tiles:
                p = getps()
                rb = rhs_bufs(c0, n) if callable(rhs_bufs) else rhs_bufs
                for k in range(nk):
                    S.op("pe", lambda e, k=k, c0=c0, n=n, p=p: e.matmul(p.t[:, 0:n], lhsT=wb.t[:, k, :], rhs=rhs_fn(k, c0, n),
                                                                      start=(k == 0), stop=(k == nk - 1)),
                         reads=[wb.b] + list(rb), writes=[p.b])
                evac(p, c0, n)

        def proj_fm_g(wb, nk, rhs_fn, rhs_bufs, tiles, evac):
            for tl in tiles:
                proj_fm(wb, nk, rhs_fn, rhs_bufs, [tl], evac)
                yield

        S.barrier()
        RR.reset()
        RL.reset()
        RC.reset()
        sb = RR.sb
        NKG = [NE + 128 * d for (_, d) in GROUPS]
        NTG = [sum(len(c[5]) for c in PLAN if c[0] == g) for g in range(3)]
        VC0 = [0, NTG[0], NTG[0] + NTG[1]]
        qTg = [sb("qT%d" % g, [128, NE], BF16) for g in range(3)]
        kTg = [sb("kT%d" % g, [128, NKG[g]], BF16) for g in range(3)]
        vTg = [sb("vT%d" % g, [128, NTG[g], 128], BF16) for g in range(3)]
        accn = sb("accn", [128, NE])
        accd = sb("accd", [128, NE])
        Ebuf = [RL.sb("E%d" % i, [128, 256]) for i in range(4)]
        Pbuf = [RL.sb("Pt%d" % i, [128, 256], BF16) for i in range(8)]
        Wmg = [RL.sb("Wm%d" % g, [128, 2, 256]) for g in range(3)]
        vtt = RL.sb("vtt", [128, 4098], BF16)
        p1b_xin = [RC.sb("xinb%d" % i, [128, D]) for i in range(3)]
        p1b_xnb = [RC.sb("xnbb%d" % i, [128, D], BF16) for i in range(2)]
        RC.reset()
        comb = RC.sb("comb", [128, 4, NE], BF16)
        SA = [P[0], P[2]]
        SBk = [P[1], P[3]]
        PNb = [P[4]]
        PDb = [P[5]]
        gen_banks[:] = [6, 7]
        erot = [0]
        prot = [0]
        ptr = [0]
        gctr = [0]

        def gen_proj(pair, g, dry=False):
            win, d = GROUPS[g]
            cq = OFF_Q + g * 512 + pair * 128
            ck = OFF_K + g * 512 + pair * 128
            cv = OFF_V + g * 512 + pair * 128
            nkc = NKG[g]
            kc0 = 1024 - 64 * d
            classes = [c for c in PLAN if c[0] == g]
            if dry:
                n_ = len(etiles(NE)) + 2 * len(etiles(nkc)) + sum((len(c[5]) + 7) // 8 for c in classes)
                for _ in range(n_):
                    yield
                return
            wq_ = load_w(w_in_v, cq)
            yield from proj_fm_g(wq_, 8, lambda k, c0, n: hT[:, k, 1024 + c0:1024 + c0 + n], lambda c0, n: hbufs(1024 + c0, n), etiles(NE),
                                 lambda p, c0, n: S.op("act", lambda e: e.activation(qTg[g].t[:, c0:c0 + n], p.t[:, 0:n], AF.Identity, bias=cpc(CP_BIN + cq // 128)),
                                                       reads=[p.b, cp.b], writes=[qTg[g].b]))
            wk_ = load_w(w_in_v, ck)
            yield from proj_fm_g(wk_, 8, lambda k, c0, n: hT[:, k, kc0 + c0:kc0 + c0 + n], lambda c0, n: hbufs(kc0 + c0, n), etiles(nkc),
                                 lambda p, c0, n: S.op("act", lambda e: e.activation(kTg[g].t[:, c0:c0 + n], p.t[:, 0:n], AF.Identity, bias=cpc(CP_BIN + ck // 128)),
                                                       reads=[p.b, cp.b], writes=[kTg[g].b]))
            wv_ = load_w(w_in_v, cv)
            yield from proj_fm_g(wv_, 8, lambda k, c0, n: hT[:, k, kc0 + c0:kc0 + c0 + n], lambda c0, n: hbufs(kc0 + c0, n), etiles(nkc),
                                 lambda p, c0, n: S.op("dve", lambda e: e.tensor_scalar(vtt.t[:, c0:c0 + n], p.t[:, 0:n], cpc(CP_BIN + cv // 128), None, ALU.add),
                                                       reads=[p.b, cp.b], writes=[vtt.b]))
            for (gg, dd, r, ia, ib, tiles) in classes:
                for t0_ in range(0, len(tiles), 8):
                    grp = tiles[t0_:t0_ + 8]
                    pt = PT[ptr[0] % 2]
                    ptr[0] += 1
                    for si, (ks, ke, vc) in enumerate(grp):
                        M = ke - ks
                        c0 = r + d * ks + 1 + 64 * d
                        S.op("pe", lambda e, si=si, M=M, c0=c0, pt=pt: e.transpose(pt.t[0:M, si, :], vtt.t[:, c0:c0 + d * (M - 1) + 1:d], ident.t[:, :]),
                             reads=[vtt.b, ident.b], writes=[pt.b])
                    nfull = sum(1 for (ks, ke, vc) in grp if ke - ks == 128)
                    vl0 = grp[0][2] - VC0[g]
                    if nfull:
                        S.op("act", lambda e, nfull=nfull, vl0=vl0, pt=pt: e.activation(vTg[g].t[:, vl0:vl0 + nfull, :], pt.t[:, 0:nfull, :], AF.Identity),
                             reads=[pt.b], writes=[vTg[g].b])
                    for si, (ks, ke, vc) in enumerate(grp):
                        M = ke - ks
                        if M < 128:
                            S.op("act", lambda e, si=si, M=M, vl=vc - VC0[g], pt=pt: e.activation(vTg[g].t[0:M, vl, :], pt.t[0:M, si, :], AF.Identity),
                                 reads=[pt.b], writes=[vTg[g].b])
                    yield

        def gen_att(pair, g, dry=False):
            win, d = GROUPS[g]
            classes = [c for c in PLAN if c[0] == g]
            if dry:
                for c in classes:
                    for _ in range(len(c[5]) + 1):
                        yield
                return
            qT_, kT_, vT_, Wm_ = qTg[g], kTg[g], vTg[g], Wmg[g]
            for hh in range(2):
                sl = SLOPES[g * 8 + pair * 2 + hh] * d
                S.op("act", lambda e, hh=hh, sl=sl: e.activation(Wm_.t[:, hh, :], cpt[:, CP_D:CP_D + 256], AF.Exp, scale=-sl),
                     reads=[cp.b], writes=[Wm_.b])
                S.op("pool", lambda e, hh=hh: e.tensor_tensor(Wm_.t[:, hh, :], Wm_.t[:, hh, :], cpt[:, CP_M:CP_M + 256], ALU.mult),
                     reads=[cp.b, Wm_.b], writes=[Wm_.b])
            for (gg, dd, r, ia, ib, tiles) in classes:
                ntile = len(tiles)
                nseg = ntile - 1
                ptl = {}
                cur = {}
                for j in range(ntile + 1):
                    if j < ntile:
                        ks, ke, vc = tiles[j]
                        M = ke - ks
                        qlo = max(ia, ks - 64)
                        qhi = min(ib, ke + 64)
                        n = qhi - qlo
                        wo = qlo - (ks - 64)
                        kc = r + d * ks + 1 + 64 * d
                        qc = r + d * qlo + 1
                        pa, pb = SA[j % 2], SBk[j % 2]
                        S.op("pe", lambda e, M=M, n=n, kc=kc, qc=qc, pa=pa: e.matmul(
                            pa.t[0:M, 0:n], lhsT=kT_.t[0:64, kc:kc + d * (M - 1) + 1:d], rhs=qT_.t[0:64, qc:qc + d * (n - 1) + 1:d],
                            start=True, stop=True, tile_position=(0, 0)), reads=[kT_.b, qT_.b], writes=[pa.b])
                        S.op("pe", lambda e, M=M, n=n, kc=kc, qc=qc, pb=pb: e.matmul(
                            pb.t[0:M, 0:n], lhsT=kT_.t[64:128, kc:kc + d * (M - 1) + 1:d], rhs=qT_.t[64:128, qc:qc + d * (n - 1) + 1:d],
                            start=True, stop=True, tile_position=(64, 0)), reads=[kT_.b, qT_.b], writes=[pb.b])
                        pts = []
                        for hh, pp in enumerate((pa, pb)):
                            E = Ebuf[erot[0] % 4]
                            erot[0] += 1
                            Pt = Pbuf[prot[0] % 8]
                            prot[0] += 1
                            S.op("act", lambda e, E=E, pp=pp, M=M, n=n: e.activation(E.t[0:M, 0:n], pp.t[0:M, 0:n], AF.Exp, scale=0.125),
                                 reads=[pp.b], writes=[E.b])
                            S.op("dve", lambda e, E=E, Pt=Pt, M=M, n=n, vc=vc, wo=wo, hh=hh: e.scalar_tensor_tensor(
                                Pt.t[0:M, wo:wo + n], E.t[0:M, 0:n], cpt[0:M, CP_VAL + vc:CP_VAL + vc + 1], Wm_.t[0:M, hh, wo:wo + n],
                                ALU.mult, ALU.mult), reads=[E.b, cp.b, Wm_.b], writes=[Pt.b])
                            pts.append(Pt)
                        ptl[j] = (pts, M, vc - VC0[g])
                    m = j - 2
                    if 0 <= m < nseg:
                        if m % 2 == 0:
                            cur["pn"] = PNb[gctr[0] % len(PNb)]
                            cur["pd"] = PDb[gctr[0] % len(PDb)]
                            gctr[0] += 1
                            cur["m0"] = m
                        pbk, pdk = cur["pn"], cur["pd"]
                        qa = ia + 128 * m
                        ns = min(128, ib - qa)
                        pc = (m % 2) * 128
                        for ci, (tj, co) in enumerate(((m, 128), (m + 1, 0))):
                            ptsj, Mj, vlj = ptl[tj]
                            for hh in range(2):
                                S.op("pe", lambda e, hh=hh, Mj=Mj, vlj=vlj, co=co, ns=ns, pc=pc, ci=ci, Ptt=ptsj[hh], pbk=pbk: e.matmul(
                                    pbk.t[hh * 64:(hh + 1) * 64, pc:pc + ns], lhsT=vT_.t[0:Mj, vlj, hh * 64:(hh + 1) * 64], rhs=Ptt.t[0:Mj, co:co + ns],
                                    start=(ci == 0), stop=(ci == 1), tile_position=(0, hh * 64)),
                                    reads=[vT_.b, ptsj[hh].b], writes=[pbk.b])
                            for hh in range(2):
                                S.op("pe", lambda e, hh=hh, Mj=Mj, co=co, ns=ns, pc=pc, ci=ci, Ptt=ptsj[hh], pdk=pdk: e.matmul(
                                    pdk.t[hh * 64:(hh + 1) * 64, pc:pc + ns], lhsT=ones.t[0:Mj, 0:64], rhs=Ptt.t[0:Mj, co:co + ns],
                                    start=(ci == 0), stop=(ci == 1), tile_position=(0, hh * 64)),
                                    reads=[ones.b, ptsj[hh].b], writes=[pdk.b])
                        if m % 2 == 1 or m == nseg - 1:
                            m0 = cur["m0"]
                            qa0 = ia + 128 * m0
                            nq = min(ib, ia + 128 * (m + 1)) - qa0
                            e0 = r + d * qa0 + 1
                            sl_ = slice(e0, e0 + d * (nq - 1) + 1, d)
                            if g == 0:
                                S.op("dve", lambda e, sl_=sl_, nq=nq, pbk=pbk: e.tensor_copy(accn.t[:, sl_], pbk.t[:, 0:nq]), reads=[pbk.b], writes=[accn.b])
                                S.op("act", lambda e, sl_=sl_, nq=nq, pdk=pdk: e.activation(accd.t[:, sl_], pdk.t[:, 0:nq], AF.Identity), reads=[pdk.b], writes=[accd.b])
                            else:
                                S.op("dve", lambda e, sl_=sl_, nq=nq, pbk=pbk: e.tensor_tensor(accn.t[:, sl_], accn.t[:, sl_], pbk.t[:, 0:nq], ALU.add),
                                     reads=[pbk.b, accn.b], writes=[accn.b])
                                S.op("dve", lambda e, sl_=sl_, nq=nq, pdk=pdk: e.tensor_tensor(accd.t[:, sl_], accd.t[:, sl_], pdk.t[:, 0:nq], ALU.add),
                                     reads=[pdk.b, accd.b], writes=[accd.b])
                    yield
            if g == 2:
                S.op("dve", lambda e: e.reciprocal(accd.t[:, :], accd.t[:, :]), reads=[accd.b], writes=[accd.b])
                S.op("dve", lambda e: e.tensor_tensor(comb.t[:, pair, :], accn.t[:, :], accd.t[:, :], ALU.mult),
                     reads=[accn.b, accd.b], writes=[comb.b])

        def merge(streams):
            streams = [(g_, n_) for (g_, n_) in streams if n_ > 0]
            if not streams:
                return
            tot = max(n_ for _, n_ in streams)
            acc = [0.0] * len(streams)
            for _ in range(tot):
                for i_, (g_, n_) in enumerate(streams):
                    acc[i_] += n_ / tot
                    while acc[i_] >= 1.0 - 1e-9:
                        next(g_, None)
                        acc[i_] -= 1.0
            for g_, _ in streams:
                for _ in g_:
                    pass

        def take(g_, n_):
            for _ in range(n_):
                next(g_, None)
                yield

        def count(gen):
            return sum(1 for _ in gen)

        units = [(p_, g_) for p_ in range(4) for g_ in range(3)]
        tail_blocks = [6, 25, 26, 5, 27, 4, 28, 3, 29, 2, 30, 1, 31, 0, 32]
        tail = gen_p1(tail_blocks, p1b_xin, p1b_xnb)
        tail_n = len(tail_blocks) + 3
        prev = None
        for ui, u in enumerate(units + [None]):
            streams = []
            if u is not None:
                streams.append((gen_proj(u[0], u[1]), count(gen_proj(u[0], u[1], dry=True))))
            if prev is not None:
                streams.append((gen_att(prev[0], prev[1]), count(gen_att(prev[0], prev[1], dry=True))))
            if ui == 0:
                streams.append((take(tail, 8), 8))
            if ui == 1:
                streams.append((take(tail, tail_n - 8), tail_n - 8))
            if _MODE == 1:
                for g_, _n in streams[::-1]:
                    for _ in g_:
                        pass
            else:
                merge(streams)
            if ui == 1:
                for _ in tail:
                    pass
                for t_ in p1b_xin + p1b_xnb:
                    for (e_, tk_) in ([t_.b.w] if t_.b.w else []) + list(t_.b.r.items()):
                        if comb.b.r.get(e_, 0) < tk_:
                            comb.b.r[e_] = tk_
            prev = u

        dump(S, "d_comb", comb)
        if debug:
            S.dma("sp", "dbg", dbg["d_hT"][:], regA.t[:], reads=hTb)
        S.barrier()
        gen_banks[:] = [0, 1, 2, 3, 4, 5]
        NCH = NE + 2
        RR.reset()
        RL.reset()
        yap = RR.sb("yap", [128, 8, NE], BF16)
        Csb = RR.sb("Csb", [128, NCH])
        Hsb = RR.sb("Hsb", [128, NCH])
        Asb = RR.sb("Asb", [128, NE])
        for i in range(8):
            wc = load_w(w_in_v, OFF_C + i * 128)
            wh = load_w(w_in_v, OFF_H + i * 128)
            wbb = load_w(w_in_v, OFF_B + i * 128)
            for (wt_, dstT, bcol) in ((wc, Csb, OFF_C // 128 + i), (wh, Hsb, OFF_H // 128 + i)):
                proj_fm(wt_, 8, lambda k, c0, n: hT[:, k, 1023 + c0:1023 + c0 + n], lambda c0, n: hbufs(1023 + c0, n), etiles(NCH),
                        lambda p, c0, n, dstT=dstT, bcol=bcol: S.op("act", lambda e: e.activation(dstT.t[:, c0:c0 + n], p.t[:, 0:n], AF.Identity, bias=cpc(CP_BIN + bcol)),
                                                                   reads=[p.b, cp.b], writes=[dstT.b]))
            S.op("dve", lambda e: e.tensor_tensor(Csb.t[:, :], Csb.t[:, :], Hsb.t[:, :], ALU.mult), reads=[Csb.b, Hsb.b], writes=[Csb.b])
            S.op("dve", lambda e: e.tensor_scalar(Csb.t[:, 1:2], Csb.t[:, 1:2], cpc(CP_FLAG), None, ALU.mult), reads=[Csb.b, cp.b], writes=[Csb.b])
            S.op("dve", lambda e: e.tensor_scalar(Csb.t[:, NE:NE + 1], Csb.t[:, NE:NE + 1], cpc(CP_FLAG + 1), None, ALU.mult), reads=[Csb.b, cp.b], writes=[Csb.b])
            S.op("dve", lambda e, i=i: e.tensor_scalar(Asb.t[:, :], Csb.t[:, 1:NE + 1], cpc(CP_CW + 8 + i), None, ALU.mult), reads=[Csb.b, cp.b], writes=[Asb.b])
            S.op("dve", lambda e, i=i: e.scalar_tensor_tensor(Asb.t[:, :], Csb.t[:, 0:NE], cpc(CP_CW + i), Asb.t[:, :], ALU.mult, ALU.add),
                 reads=[Csb.b, cp.b, Asb.b], writes=[Asb.b])
            S.op("dve", lambda e, i=i: e.scalar_tensor_tensor(Asb.t[:, :], Csb.t[:, 2:NE + 2], cpc(CP_CW + 16 + i), Asb.t[:, :], ALU.mult, ALU.add),
                 reads=[Csb.b, cp.b, Asb.b], writes=[Asb.b])
            proj_fm(wbb, 8, lambda k, c0, n: hT[:, k, 1024 + c0:1024 + c0 + n], lambda c0, n: hbufs(1024 + c0, n), etiles(NE),
                    lambda p, c0, n, i=i: S.op("dve", lambda e: e.scalar_tensor_tensor(yap.t[:, i, c0:c0 + n], p.t[:, 0:n], cpc(CP_BIN + OFF_B // 128 + i), Asb.t[:, c0:c0 + n], ALU.add, ALU.mult),
                                               reads=[p.b, cp.b, Asb.b], writes=[yap.b]))

        dump(S, "d_yap", yap)
        S.barrier()
        RR.cur = RR.lo + 32800
        RL.reset()
        mg = RR.sb("merged", [128, 8, NE], BF16)
        sga = [RL.sb("sga%d" % i, [128, 512]) for i in range(2)]
        sgb = [RL.sb("sgb%d" % i, [128, 512]) for i in range(2)]
        tmg = [RL.sb("tmg%d" % i, [128, 512]) for i in range(2)]
        tmh = [RL.sb("tmh%d" % i, [128, 512]) for i in range(2)]
        it = [0]
        for i in range(8):
            wga = load_w(w_in_v, OFF_GA + i * 128)
            wgb = load_w(w_in_v, OFF_GB + i * 128)
            wa = load_w(w_a_v, i * 128)
            wbm = load_w(w_b_v, i * 128, nk=4)
            for (c0, n) in etiles(NE):
                q = it[0] % 2
                it[0] += 1
                A_, B_, t1_, t2_ = sga[q], sgb[q], tmg[q], tmh[q]
                proj_fm(wga, 8, lambda k, c0_, n_: hT[:, k, 1024 + c0_:1024 + c0_ + n_], lambda c0_, n_: hbufs(1024 + c0_, n_), [(c0, n)],
                        lambda p, c0_, n_, i=i, A_=A_: S.op("act", lambda e: e.activation(A_.t[:, 0:n_], p.t[:, 0:n_], AF.Sigmoid, bias=cpc(CP_BIN + OFF_GA // 128 + i)),
                                                            reads=[p.b, cp.b], writes=[A_.b]))
                proj_fm(wa, 8, lambda k, c0_, n_: yap.t[:, k, c0_:c0_ + n_], [yap.b], [(c0, n)],
                        lambda p, c0_, n_, A_=A_, t1_=t1_: S.op("dve", lambda e: e.tensor_tensor(t1_.t[:, 0:n_], p.t[:, 0:n_], A_.t[:, 0:n_], ALU.mult),
                                                               reads=[p.b, A_.b], writes=[t1_.b]))
                proj_fm(wgb, 8, lambda k, c0_, n_: hT[:, k, 1024 + c0_:1024 + c0_ + n_], lambda c0_, n_: hbufs(1024 + c0_, n_), [(c0, n)],
                        lambda p, c0_, n_, i=i, B_=B_: S.op("act", lambda e: e.activation(B_.t[:, 0:n_], p.t[:, 0:n_], AF.Sigmoid, bias=cpc(CP_BIN + OFF_GB // 128 + i)),
                                                            reads=[p.b, cp.b], writes=[B_.b]))
                proj_fm(wbm, 4, lambda k, c0_, n_: comb.t[:, k, c0_:c0_ + n_], [comb.b], [(c0, n)],
                        lambda p, c0_, n_, B_=B_, t2_=t2_: S.op("dve", lambda e: e.tensor_tensor(t2_.t[:, 0:n_], p.t[:, 0:n_], B_.t[:, 0:n_], ALU.mult),
                                                               reads=[p.b, B_.b], writes=[t2_.b]))
                S.op("pool", lambda e, i=i, c0=c0, n=n, t1_=t1_, t2_=t2_: e.tensor_tensor(mg.t[:, i, c0:c0 + n], t1_.t[:, 0:n], t2_.t[:, 0:n], ALU.add),
                     reads=[t1_.b, t2_.b], writes=[mg.b])

        dump(S, "d_mg", mg)
        S.barrier()
        sb = R2.sb
        wo_k = [sb("wo_sb%d" % k, [128, D], BF16) for k in range(8)]
        for k in range(8):
            S.dma("pool", "w", wo_k[k].t[:, :], w_o_v[:, k, :], writes=[wo_k[k].b])
        bcB = sb("bcB", [128, 3, D])
        S.dma("sp", "c", bcB.t[:, 0, :], bc["b_o"][:], writes=[bcB.b])
        S.dma("sp", "c", bcB.t[:, 1, :], bc["ln1_g"][:], writes=[bcB.b])
        S.dma("sp", "c", bcB.t[:, 2, :], bc["ln1_b"][:], writes=[bcB.b])
        hres = [sb("hres%d" % i, [128, D]) for i in range(4)]
        zt = [sb("zt%d" % i, [128, D]) for i in range(3)]
        znt = [sb("znt%d" % i, [128, D]) for i in range(3)]
        tmp1 = [sb("tmp1_%d" % i, [128, D]) for i in range(2)]
        hb = [sb("hb%d" % i, [128, D], BF16) for i in range(3)]
        h1T = at("h1T", [128, 8, NE], BF16, RR.lo)
        hs_b = Buf("h_scr")
        p5ps = {}

        def blkM(b):
            return 128 if b < 16 else 2

        def p5_s0(b):
            M = blkM(b)
            S.dma("sp", "hl", hres[b % 4].t[0:M, :], h_scr[b * 128:b * 128 + M, :], reads=[hs_b], writes=[hres[b % 4].b])

        def p5_s1(b):
            M, e0 = blkM(b), b * 128
            p5ps[b] = []
            for half in range(2):
                p = getps()
                p5ps[b].append(p)
                for k in range(8):
                    S.op("pe", lambda e, k=k, p=p, half=half: e.matmul(p.t[0:M, :], lhsT=mg.t[:, k, e0:e0 + M], rhs=wo_k[k].t[:, half * 512:(half + 1) * 512],
                                                                   start=(k == 0), stop=(k == 7)), reads=[mg.b, wo_k[k].b], writes=[p.b])

        def p5_s2(b):
            M = blkM(b)
            hr, z = hres[b % 4], zt[b % 3]
            for half in range(2):
                p = p5ps[b][half]
                S.op("dve", lambda e, p=p, half=half: e.tensor_tensor(z.t[0:M, half * 512:(half + 1) * 512], hr.t[0:M, half * 512:(half + 1) * 512], p.t[0:M, :], ALU.add),
                     reads=[p.b, hr.b], writes=[z.b])
            ln_stats_a(z, M, b)

        def p5_s3(b):
            M = blkM(b)
            z, zn = zt[b % 3], znt[b % 3]
            ln_stats_b(z, zn, M, b)
            S.dma("act", "zs", zn_scr[b * 128:b * 128 + M, :], zn.t[0:M, :], reads=[zn.b])

        def p5_s4(b):
            M = blkM(b)
            zn, t1, h = znt[b % 3], tmp1[b % 2], hb[b % 3]
            S.op("dve" if b % 2 == 0 else "pool", lambda e: e.tensor_tensor(t1.t[0:M, :], zn.t[0:M, :], bcB.t[0:M, 1, :], ALU.mult), reads=[zn.b, bcB.b], writes=[t1.b])
            S.op("pool", lambda e: e.tensor_tensor(h.t[0:M, :], t1.t[0:M, :], bcB.t[0:M, 2, :], ALU.add), reads=[t1.b, bcB.b], writes=[h.b])

        def p5_s5(b):
            M = blkM(b)
            transpose_to(hb[b % 3], M, h1T.t[:, :, b * 128:b * 128 + M], h1T.b, b)

        run_pipeline(17, [p5_s0, p5_s1, p5_s2, p5_s3, p5_s4, p5_s5])
        dump(S, "d_h1T", h1T)
        S.dma("sp", "c", bcA.t[:, 3, :], bc["b_down"][:], writes=[bcA.b])
        S.op("dve", lambda e: e.scalar_tensor_tensor(bcA.t[:, 3, :], bcB.t[:, 2, :], ALPHA, bcA.t[:, 3, :], ALU.mult, ALU.add), reads=[bcA.b, bcB.b], writes=[bcA.b])
        S.op("dve", lambda e: e.tensor_scalar(bcA.t[:, 2, :], bcB.t[:, 1, :], ALPHA, None, ALU.mult), reads=[bcA.b, bcB.b], writes=[bcA.b])

        S.barrier()
        zs_b = Buf("zn_scr")
        S.dma("sp", "c", bcA.t[:, 0, :], bc["ln2_g"][:], writes=[bcA.b])
        S.dma("sp", "c", bcA.t[:, 1, :], bc["ln2_b"][:], writes=[bcA.b])
        fT = at("fT", [128, 22, 1024], BF16, RR.lo + 32800)
        R2.lo = R2.cur = RR.lo + 32800 + 45056
        sb = R2.sb
        wd_k = [sb("wd_sb%d" % k, [128, D], BF16) for k in range(22)]
        hres = [sb("hres%d" % i, [128, D]) for i in range(3)]
        zt = [sb("zt%d" % i, [128, D]) for i in range(3)]
        znt = [sb("znt%d" % i, [128, D]) for i in range(2)]
        NA = 1026
        asb = [sb("asb%d" % i, [128, NA]) for i in range(2)]
        csb = [sb("csb%d" % i, [128, 1024]) for i in range(2)]
        for half in range(2):
            eb = half * 1024
            for i in range(22):
                wa_ = load_w(w_up_v, i * 128)
                wg_ = load_w(w_up_v, DFF + i * 128)
                if half == 0 and i >= 2:
                    for k_ in ((i - 2, i - 1) if i == 21 else (i - 2,)) if i < 21 else (19, 20, 21):
                        S.dma("pool", "w", wd_k[k_].t[:, :], w_down_v[:, k_, :], writes=[wd_k[k_].b])
                a_, c_ = asb[i % 2], csb[i % 2]
                g_ = c_
                proj_fm(wa_, 8, lambda k, c0, n, eb=eb: h1T.t[:, k, eb + c0:eb + c0 + n], [h1T.b], etiles(NA),
                        lambda p, c0, n, a_=a_, i=i: S.op("act", lambda e: e.activation(a_.t[:, c0:c0 + n], p.t[:, 0:n], AF.Identity, bias=cpc(CP_BUP + i)),
                                                          reads=[p.b, cp.b], writes=[a_.b]))
                if half == 0:
                    S.op("dve", lambda e, a_=a_: e.tensor_scalar(a_.t[:, 0:1], a_.t[:, 0:1], cpc(CP_FLAG), None, ALU.mult), reads=[a_.b, cp.b], writes=[a_.b])
                else:
                    S.op("dve", lambda e, a_=a_: e.tensor_scalar(a_.t[:, NA - 1:NA], a_.t[:, NA - 1:NA], cpc(CP_FLAG + 1), None, ALU.mult), reads=[a_.b, cp.b], writes=[a_.b])
                S.op("dve", lambda e, a_=a_, c_=c_, i=i: e.tensor_scalar(c_.t[:, :], a_.t[:, 1:1025], cpc(CP_FCW + 22 + i), cpc(CP_FCB + i), ALU.mult, ALU.add),
                     reads=[a_.b, cp.b], writes=[c_.b])
                S.op("dve", lambda e, a_=a_, c_=c_, i=i: e.scalar_tensor_tensor(c_.t[:, :], a_.t[:, 0:1024], cpc(CP_FCW + i), c_.t[:, :], ALU.mult, ALU.add),
                     reads=[a_.b, cp.b, c_.b], writes=[c_.b])
                S.op("dve", lambda e, a_=a_, c_=c_, i=i: e.scalar_tensor_tensor(c_.t[:, :], a_.t[:, 2:1026], cpc(CP_FCW + 44 + i), c_.t[:, :], ALU.mult, ALU.add),
                     reads=[a_.b, cp.b, c_.b], writes=[c_.b])
                S.op("act", lambda e, c_=c_: e.activation(c_.t[:, :], c_.t[:, :], AF.Gelu), reads=[c_.b], writes=[c_.b])
                proj_fm(wg_, 8, lambda k, c0, n, eb=eb: h1T.t[:, k, eb + 1 + c0:eb + 1 + c0 + n], [h1T.b], etiles(1024),
                        lambda p, c0, n, g_=g_, i=i: S.op("dve", lambda e: e.scalar_tensor_tensor(fT.t[:, i, c0:c0 + n], p.t[:, 0:n], cpc(CP_BUP + 22 + i), g_.t[:, c0:c0 + n], ALU.add, ALU.mult),
                                                          reads=[p.b, cp.b, g_.b], writes=[fT.b]))
            p6ps = {}

            def p6_s0(tb, eb=eb):
                erow = eb + 1 + tb * 128
                S.dma("sp", "zl", hres[tb % 3].t[:, :], zn_scr[erow:erow + 128, :], reads=[zs_b], writes=[hres[tb % 3].b])

            def p6_s1(tb):
                zr, t0 = hres[tb % 3], tb * 128
                S.op("pool", lambda e: e.tensor_tensor(zr.t[:, :], zr.t[:, :], bcA.t[:, 2, :], ALU.mult), reads=[zr.b, bcA.b], writes=[zr.b])
                S.op("pool", lambda e: e.tensor_tensor(zr.t[:, :], zr.t[:, :], bcA.t[:, 3, :], ALU.add), reads=[zr.b, bcA.b], writes=[zr.b])
                p6ps[tb] = []
                for hf_ in range(2):
                    p = getps()
                    p6ps[tb].append(p)
                    for k in range(22):
                        S.op("pe", lambda e, k=k, p=p, hf_=hf_: e.matmul(p.t[:, :], lhsT=fT.t[:, k, t0:t0 + 128], rhs=wd_k[k].t[:, hf_ * 512:(hf_ + 1) * 512],
                                                                     start=(k == 0), stop=(k == 21)), reads=[fT.b, wd_k[k].b], writes=[p.b])

            def p6_s2(tb):
                zr, z = hres[tb % 3], zt[tb % 3]
                for hf_ in range(2):
                    p = p6ps[tb][hf_]
                    S.op("dve", lambda e, p=p, hf_=hf_: e.tensor_tensor(z.t[:, hf_ * 512:(hf_ + 1) * 512], zr.t[:, hf_ * 512:(hf_ + 1) * 512], p.t[:, :], ALU.add),
                         reads=[p.b, zr.b], writes=[z.b])
                ln_stats_a(z, 128, tb)

            def p6_s3(tb):
                ln_stats_b(zt[tb % 3], znt[tb % 2], 128, tb)

            def p6_s4(tb, half=half):
                zn, o_ = znt[tb % 2], zt[tb % 3]
                S.op("dve", lambda e: e.tensor_tensor(zn.t[:, :], zn.t[:, :], bcA.t[:, 0, :], ALU.mult), reads=[zn.b, bcA.b], writes=[zn.b])
                S.op("pool", lambda e: e.tensor_tensor(o_.t[:, :], zn.t[:, :], bcA.t[:, 1, :], ALU.add), reads=[zn.b, bcA.b], writes=[o_.b])
                r0 = half * 1024 + tb * 128
                S.dma("pool", "o", out[r0:r0 + 128, :], o_.t[:, :], reads=[o_.b])

            run_pipeline(8, [p6_s0, p6_s1, p6_s2, p6_s3, p6_s4])
        for nm_, och in S.chans.items():
            if nm_.startswith("o_"):
                S.prog["sp"].append(("wait", och.sem, och.count))
        S.emit()
    return nc


_CACHE = {}


def _host_consts():
    D_ = np.zeros((128, 256), np.float32)
    M_ = np.zeros((128, 256), np.float32)
    for k in range(128):
        for q in range(256):
            dist = abs(q - 64 - k)
            D_[k, q] = min(dist, 64)
            M_[k, q] = 1.0 if dist <= 64 else 0.0
    return D_, M_


def kernel(x, ln0_g, ln0_b, w_in, b_in, conv_w, w_a, w_b, w_o, b_o, ln1_g, ln1_b,
           w_up, b_up, ffn_conv_w, ffn_conv_b, w_down, b_down, ln2_g, ln2_b):
    f = lambda a: np.ascontiguousarray(np.asarray(a, dtype=np.float32))
    x = f(x)
    if "nc" not in _CACHE:
        _CACHE["nc"] = build_program(_CACHE.get("debug", False))
    nc = _CACHE["nc"]
    attD, attM = _host_consts()
    bcast = lambda v: np.ascontiguousarray(np.broadcast_to(f(v).reshape(1, -1), (128, f(v).size)))
    common = {
        "w_in": f(w_in)[0], "w_a": f(w_a)[0], "w_b": f(w_b)[0], "w_o": f(w_o)[0], "w_up": f(w_up)[0], "w_down": f(w_down)[0],
        "ident": np.eye(128, dtype=np.float32),
        "bc_ln0_g": bcast(ln0_g), "bc_ln0_b": bcast(ln0_b), "bc_b_v": bcast(f(b_in)[0, OFF_V:OFF_V + 1536]),
        "bc_b_o": bcast(b_o), "bc_ln1_g": bcast(ln1_g), "bc_ln1_b": bcast(ln1_b), "bc_b_down": bcast(b_down),
        "bc_ln2_g": bcast(ln2_g), "bc_ln2_b": bcast(ln2_b),
    }
    cp0 = np.zeros((128, NCP), np.float32)
    cp0[:, CP_BIN:CP_BIN + 76] = f(b_in)[0].reshape(76, 128).T
    cp0[:, CP_CW:CP_CW + 24] = f(conv_w)[0].reshape(3, 8, 128).transpose(2, 0, 1).reshape(128, 24)
    cp0[:, CP_BUP:CP_BUP + 44] = f(b_up)[0].reshape(44, 128).T
    cp0[:, CP_FCW:CP_FCW + 66] = f(ffn_conv_w)[0].reshape(3, 22, 128).transpose(2, 0, 1).reshape(128, 66)
    cp0[:, CP_FCB:CP_FCB + 22] = f(ffn_conv_b)[0].reshape(22, 128).T
    cp0[:, CP_D:CP_D + 256] = attD
    cp0[:, CP_M:CP_M + 256] = attM
    cp0[:, CP_EPS] = LN_EPS
    cp0[:, CP_G0:CP_G0 + 8] = f(ln0_g).reshape(8, 128).T
    cp0[:, CP_B0:CP_B0 + 8] = f(ln0_b).reshape(8, 128).T
    in_maps = []
    for c in range(NCORE):
        b, s0 = c // 4, (c % 4) * TPC
        xe = np.zeros((NX, D), np.float32)
        lo, hi = s0 - HL, s0 - HL + NX
        a, bb = max(lo, 0), min(hi, SEQ)
        xe[a - lo:bb - lo] = x[b, a:bb]
        cpc_ = cp0.copy()
        cpc_[:, CP_FLAG] = 1.0 if s0 > 0 else 0.0
        cpc_[:, CP_FLAG + 1] = 1.0 if s0 + TPC < SEQ else 0.0
        for (g, d, r, ia, ib, tiles) in PLAN:
            for (ks, ke, vc) in tiles:
                tok = s0 + r + d * (ks + np.arange(128))
                cpc_[:, CP_VAL + vc] = ((tok >= 0) & (tok < SEQ) & (np.arange(128) < ke - ks)).astype(np.float32)
        m = dict(common)
        m["x_ext"] = xe
        m["cpart"] = cpc_
        in_maps.append(m)
    res = run_bass_kernel_spmd(nc, in_maps, core_ids=list(range(NCORE)))
    _CACHE["res"] = res
    outp = np.zeros((2, SEQ, D), np.float32)
    for c in range(NCORE):
        b, s0 = c // 4, (c % 4) * TPC
        outp[b, s0:s0 + TPC] = res.results[c]["out"]
    return outp
```
